# Optimizing a Trainium2 kernel written in Bass

```python
import math
import jax, jax.numpy as jnp
from jax import lax
import numpy as np

D_MODEL = 1024
BATCH = 8
SEQ = 8192
DEPTH = 2
DEC_BATCH = 16
DEC_SEQ = 16
PAST_LEN = 1024

CHUNK = 64
N_A_LAYERS = DEPTH // 2
N_B_LAYERS = DEPTH - N_A_LAYERS
HG_EXPAND = 128
HG_HEADS = D_MODEL // HG_EXPAND
HG_K = HG_EXPAND
HG_V = D_MODEL // HG_HEADS
DA_HEADS = 8
DA_HEAD_DIM = D_MODEL // (2 * DA_HEADS)
DA_V_DIM = 2 * DA_HEAD_DIM
D_FF = 4 * D_MODEL
Q_BLOCK = 128
NORM_EPS = 1e-6
LAMBDA_STD = 0.1

kernel_name = 'yoco_hgrn2_diffattn_stream_step'


def _rmsnorm(x, g):
    x32 = x.astype(jnp.float32)
    y = x32 * lax.rsqrt(jnp.mean(jnp.square(x32), axis=-1, keepdims=True) + NORM_EPS)
    return (y * g.astype(jnp.float32)).astype(x.dtype)


def _sqrelu_mlp(x, w_up, w_down):
    return jnp.square(jax.nn.relu(x @ w_up)) @ w_down


def _gla_chunk_step(S, xs):
    q, k, v, g = xs
    C = q.shape[1]
    b = jnp.cumsum(g, axis=1)
    causal = jnp.tril(jnp.ones((C, C), dtype=bool))
    diff = b[:, :, None] - b[:, None, :]
    decay = jnp.exp(jnp.where(causal[None, :, :, None, None], diff, -jnp.inf))
    a = jnp.einsum('bthk,bshk,btshk->bhts', q, k, decay)
    o = jnp.einsum('bhts,bshv->bthv', a, v) + jnp.einsum('bthk,bhkv->bthv', q * jnp.exp(b), S)
    b_last = b[:, -1]
    S_new = jnp.exp(b_last)[..., None] * S + jnp.einsum(
        'bshk,bshv->bhkv', k * jnp.exp(b_last[:, None] - b), v)
    return S_new, o


def _hgrn2(a, s0, w_in, lb, onorm_g, w_o, is_prompt):
    B, T, _ = a.shape
    q, f, i, g = jnp.split(a @ w_in, 4, axis=-1)
    fg = lb + (1.0 - lb) * jax.nn.sigmoid(f.astype(jnp.float32))
    shp = (B, T, HG_HEADS, HG_K)
    qh = jax.nn.silu(q.astype(jnp.float32)).reshape(shp)
    kh = (1.0 - fg).reshape(shp)
    gh = jnp.log(fg).reshape(shp)
    vh = i.astype(jnp.float32).reshape(B, T, HG_HEADS, HG_V)
    S0 = s0.astype(jnp.float32)
    if is_prompt:
        nc = T // CHUNK
        def to_chunks(z):
            return z.reshape((B, nc, CHUNK) + z.shape[2:]).swapaxes(0, 1)
        S_new, o = lax.scan(_gla_chunk_step, S0, (to_chunks(qh), to_chunks(kh), to_chunks(vh), to_chunks(gh)))
        o = o.swapaxes(0, 1).reshape(B, T, HG_HEADS, HG_V)
    else:
        S_new, o = _gla_chunk_step(S0, (qh, kh, vh, gh))
    gate = jax.nn.silu(g.astype(jnp.float32)).reshape(B, T, HG_HEADS, HG_V)
    o = _rmsnorm(o, onorm_g.reshape(HG_HEADS, HG_V)) * gate
    return o.reshape(B, T, D_MODEL).astype(a.dtype) @ w_o, S_new


def _chunk_mask(qpos, kpos):
    return (kpos[None, :] // CHUNK) <= (qpos[:, None] // CHUNK)


def _diff_core(q, k, v, mask, lam):
    s = jnp.einsum('bqhcd,bkhcd->bhcqk', q, k).astype(jnp.float32) * (DA_HEAD_DIM ** -0.5)
    s = jnp.where(mask, s, -jnp.inf)
    p = jax.nn.softmax(s, axis=-1)
    w = p[:, :, 0] - lam * p[:, :, 1]
    return jnp.einsum('bhqk,bkhe->bqhe', w.astype(v.dtype), v)


def _diff_attn(a, k_new, v_new, past_k, past_v, w_q, lam_p, subln_g, w_o, layer_idx):
    B, T, _ = a.shape
    q = (a @ w_q).reshape(B, T, DA_HEADS, 2, DA_HEAD_DIM)
    lam_init = 0.8 - 0.6 * math.exp(-0.3 * layer_idx)
    lp = lam_p.astype(jnp.float32)
    lam = jnp.exp(jnp.sum(lp[0] * lp[1])) - jnp.exp(jnp.sum(lp[2] * lp[3])) + lam_init
    if past_k is None:
        nb = T // Q_BLOCK
        qb = q.reshape(B, nb, Q_BLOCK, DA_HEADS, 2, DA_HEAD_DIM).swapaxes(0, 1)
        kpos = jnp.arange(T)
        def blk(args):
            qi, start = args
            qpos = start + jnp.arange(Q_BLOCK)
            return _diff_core(qi, k_new, v_new, _chunk_mask(qpos, kpos), lam)
        o = lax.map(blk, (qb, jnp.arange(nb) * Q_BLOCK))
        o = o.swapaxes(0, 1).reshape(B, T, DA_HEADS, DA_V_DIM)
    else:
        P = past_k.shape[1]
        k = jnp.concatenate([past_k.astype(k_new.dtype), k_new], axis=1)
        v = jnp.concatenate([past_v.astype(v_new.dtype), v_new], axis=1)
        mask = _chunk_mask(P + jnp.arange(T), jnp.arange(P + T))
        o = _diff_core(q, k, v, mask, lam)
    o = _rmsnorm(o, subln_g) * (1.0 - lam_init)
    return o.reshape(B, T, D_MODEL) @ w_o


def _trunk(x, hg_state0, past_k, past_v, norm_g, w_hgrn_in, hgrn_lb_logits, hgrn_onorm_g,
           w_hgrn_out, kv_norm_g, w_kv, w_dq, diff_lambda, diff_subln_g, w_do, w_up, w_down):
    is_prompt = past_k is None
    B, T, _ = x.shape
    lb_all = jnp.cumsum(jax.nn.softmax(hgrn_lb_logits.astype(jnp.float32), axis=0), axis=0)
    h = x
    new_states = []
    k_new = None
    v_new = None
    for l in range(DEPTH):
        a = _rmsnorm(h, norm_g[l, 0])
        if l < N_A_LAYERS:
            m, s_new = _hgrn2(a, hg_state0[l], w_hgrn_in[l], lb_all[l], hgrn_onorm_g[l],
                              w_hgrn_out[l], is_prompt)
            new_states.append(s_new.astype(hg_state0.dtype))
        else:
            j = l - N_A_LAYERS
            if j == 0:
                kv = _rmsnorm(h, kv_norm_g) @ w_kv
                k_new = kv[..., :D_MODEL].reshape(B, T, DA_HEADS, 2, DA_HEAD_DIM)
                v_new = kv[..., D_MODEL:].reshape(B, T, DA_HEADS, DA_V_DIM)
            m = _diff_attn(a, k_new, v_new, past_k, past_v, w_dq[j], diff_lambda[j],
                           diff_subln_g[j], w_do[j], l)
        h = h + _rmsnorm(m, norm_g[l, 1])
        f = _sqrelu_mlp(_rmsnorm(h, norm_g[l, 2]), w_up[l], w_down[l])
        h = h + _rmsnorm(f, norm_g[l, 3])
    return h, k_new, v_new, jnp.stack(new_states)


def setup_inputs(seed: int = 0) -> dict:
    key = jax.random.key(seed)
    ks = jax.random.split(key, 20)
    f32 = jnp.float32
    def w(k, shape, fan_in):
        return jax.random.normal(k, shape, f32) * (fan_in ** -0.5)
    def gain(k, shape):
        return 1.0 + 0.01 * jax.random.normal(k, shape, f32)
    return {
        'x_prompt': jax.random.normal(ks[0], (BATCH, SEQ, D_MODEL), f32),
        'x_sample': jax.random.normal(ks[1], (DEC_BATCH, DEC_SEQ, D_MODEL), f32),
        'cache_k': jax.random.normal(ks[2], (DEC_BATCH, PAST_LEN, DA_HEADS, 2, DA_HEAD_DIM), f32),
        'cache_v': jax.random.normal(ks[3], (DEC_BATCH, PAST_LEN, DA_HEADS, DA_V_DIM), f32),
        'state_hgrn': 0.5 * jax.random.normal(ks[4], (N_A_LAYERS, DEC_BATCH, HG_HEADS, HG_K, HG_V), f32),
        'norm_g': gain(ks[5], (DEPTH, 4, D_MODEL)),
        'w_hgrn_in': w(ks[6], (N_A_LAYERS, D_MODEL, 4 * D_MODEL), D_MODEL),
        'hgrn_lb_logits': 0.1 * jax.random.normal(ks[7], (N_A_LAYERS + 1, D_MODEL), f32),
        'hgrn_onorm_g': gain(ks[8], (N_A_LAYERS, D_MODEL)),
        'w_hgrn_out': w(ks[9], (N_A_LAYERS, D_MODEL, D_MODEL), D_MODEL),
        'kv_norm_g': gain(ks[10], (D_MODEL,)),
        'w_kv': w(ks[11], (D_MODEL, 2 * D_MODEL), D_MODEL),
        'w_dq': w(ks[12], (N_B_LAYERS, D_MODEL, D_MODEL), D_MODEL),
        'diff_lambda': LAMBDA_STD * jax.random.normal(ks[13], (N_B_LAYERS, 4, DA_HEAD_DIM), f32),
        'diff_subln_g': gain(ks[14], (N_B_LAYERS, DA_V_DIM)),
        'w_do': w(ks[15], (N_B_LAYERS, D_MODEL, D_MODEL), D_MODEL),
        'w_up': w(ks[16], (DEPTH, D_MODEL, D_FF), D_MODEL),
        'w_down': w(ks[17], (DEPTH, D_FF, D_MODEL), D_FF),
    }


def reference(x_prompt, x_sample, cache_k, cache_v, state_hgrn, norm_g, w_hgrn_in, hgrn_lb_logits,
              hgrn_onorm_g, w_hgrn_out, kv_norm_g, w_kv, w_dq, diff_lambda, diff_subln_g, w_do,
              w_up, w_down):
    s0 = jnp.zeros((N_A_LAYERS, x_prompt.shape[0], HG_HEADS, HG_K, HG_V), x_prompt.dtype)
    y_prompt, k_prompt, v_prompt, st_prompt = _trunk(
        x_prompt, s0, None, None, norm_g, w_hgrn_in, hgrn_lb_logits, hgrn_onorm_g, w_hgrn_out,
        kv_norm_g, w_kv, w_dq, diff_lambda, diff_subln_g, w_do, w_up, w_down)
    y_sample, k_sample, v_sample, st_sample = _trunk(
        x_sample, state_hgrn, cache_k, cache_v, norm_g, w_hgrn_in, hgrn_lb_logits, hgrn_onorm_g,
        w_hgrn_out, kv_norm_g, w_kv, w_dq, diff_lambda, diff_subln_g, w_do, w_up, w_down)
    return (y_prompt, y_sample, k_prompt, v_prompt, st_prompt, k_sample, v_sample, st_sample)
```

```python
import math
import os
KDBG = 'nopool'
KCUT = int(os.environ.get('KCUT', '99'))
KCUT2 = int(os.environ.get('KCUT2', '99'))
from contextlib import ExitStack
import numpy as np
import concourse.bass as bass
import concourse.mybir as mybir
from concourse.bass_utils import run_bass_kernel_spmd

F32 = mybir.dt.float32
BF16 = mybir.dt.bfloat16
AF = mybir.ActivationFunctionType
ALU = mybir.AluOpType
AX = mybir.AxisListType

TT = 512
EPS = 1e-6
LAM_INIT = 0.8 - 0.6 * math.exp(-0.3 * 1)
NW = 3
NPIECE = 50
STRICT = True


class Buf:
    __slots__ = ("name", "w", "r", "excl")

    def __init__(self, name, excl=False):
        self.name = name
        self.w = None
        self.r = []
        self.excl = excl


class Eng:
    def __init__(self, name, in_order_safe=False):
        self.name = name
        self.ops = []
        self.sem = name
        self.count = 0
        self.seen = {}
        self.in_order_safe = in_order_safe


class Sched:
    def __init__(self):
        self.E = {"pe": Eng("pe", True), "act": Eng("act"), "dve": Eng("dve"), "pool": Eng("pool"), "sp": Eng("sp")}
        self.sems = {}
        self.dma_cnt = {}
        self.n_ops = 0
        self.n_waits = 0

    def _need(self, eng, evs):
        need = {}
        for (k, v) in evs:
            if eng.seen.get(k, 0) >= v:
                continue
            if need.get(k, 0) < v:
                need[k] = v
        return need

    def _emit_waits(self, eng, need):
        for k, v in need.items():
            h = self.sems[k]
            eng.ops.append(lambda e, h=h, v=v: e.wait_ge(h, v))
            eng.seen[k] = v
            self.n_waits += 1

    def _record(self, ev, reads, writes):
        for b in reads:
            b.r.append(ev)
            if len(b.r) > 64:
                mx = {}
                for (k, v) in b.r:
                    if mx.get(k, 0) < v:
                        mx[k] = v
                b.r = list(mx.items())
        for b in writes:
            b.w = ev
            b.r = []

    def op(self, ename, fn, reads=(), writes=(), inc=True):
        if ename == "pool" and "nopool" in KDBG:
            ename = "dve"
        eng = self.E[ename]
        if ename != "pe":
            ex = [b for b in reads if b.excl]
            if ex:
                reads = [b for b in reads if not b.excl]
                writes = list(writes) + ex
        evs = []
        for b in reads:
            if b.w is not None:
                evs.append(b.w)
        for b in writes:
            if b.w is not None and (STRICT or b.w[0] != eng.sem):
                evs.append(b.w)
            for ev in b.r:
                if STRICT or ev[0] != eng.sem:
                    evs.append(ev)
        if eng.in_order_safe:
            evs = [e for e in evs if e[0] != eng.sem]
        self._emit_waits(eng, self._need(eng, evs))
        self.n_ops += 1
        if inc:
            eng.count += 1
            ev = (eng.sem, eng.count)
            h = self.sems[eng.sem]
            eng.ops.append(lambda e, fn=fn, h=h: fn(e).then_inc(h, 1))
        else:
            ev = (eng.sem, eng.count + 1)
            eng.ops.append(lambda e, fn=fn: fn(e))
        self._record(ev, reads, writes)
        return ev

    def dma(self, qname, semkey, fn, reads=(), writes=()):
        eng = self.E[qname]
        evs = []
        for b in reads:
            if b.w is not None:
                evs.append(b.w)
        for b in writes:
            if b.w is not None:
                evs.append(b.w)
            evs.extend(b.r)
        self._emit_waits(eng, self._need(eng, evs))
        self.dma_cnt[semkey] = self.dma_cnt.get(semkey, 0) + 16
        ev = (semkey, self.dma_cnt[semkey])
        h = self.sems[semkey]
        eng.ops.append(lambda e, fn=fn, h=h: fn(e).then_inc(h, 16))
        self.n_ops += 1
        self._record(ev, reads, writes)
        return ev

    def wait_event(self, ename, ev):
        eng = self.E[ename]
        self._emit_waits(eng, self._need(eng, [ev]))

    def replay(self, block):
        E = self.E

        @block.tensor
        def _(e):
            for f in E["pe"].ops:
                f(e)

        @block.scalar
        def _(e):
            for f in E["act"].ops:
                f(e)

        @block.vector
        def _(e):
            for f in E["dve"].ops:
                f(e)

        @block.gpsimd
        def _(e):
            for f in E["pool"].ops:
                f(e)

        @block.sync
        def _(e):
            for f in E["sp"].ops:
                f(e)


def build_nc(SEQ, STOP=99):
    NTILE = SEQ // TT
    nc = bass.Bass("TRN2", target_bir_lowering=False)

    def din(name, shape, dt=F32):
        return nc.dram_tensor(name, list(shape), dt, kind="ExternalInput").ap()

    def dout(name, shape):
        return nc.dram_tensor(name, list(shape), F32, kind="ExternalOutput").ap()

    xp = din("xp", [SEQ, 1024]); xs = din("xs", [32, 1024])
    ck = din("ck", [2, 1024, 1024]); cv = din("cv", [2, 1024, 1024]); sth = din("sth", [2, 8, 128, 128])
    norm_g = din("norm_g", [8, 1024]); w_in = din("w_in", [1024, 4096]); lbl = din("lbl", [2, 1024])
    onorm = din("onorm", [1024]); w_ho = din("w_ho", [1024, 1024]); kvg = din("kvg", [1024])
    w_kv = din("w_kv", [1024, 2048]); w_dq = din("w_dq", [1024, 1024]); dlam = din("dlam", [256])
    subg = din("subg", [128]); w_do = din("w_do", [1024, 1024])
    w_up = din("w_up", [2, 1024, 4096]); w_down = din("w_down", [2, 4096, 1024])
    c_ident = din("c_ident", [128, 128]); c_scanm = din("c_scanm", [128, 512]); c_mask2 = din("c_mask2", [128, 512])

    yp = dout("yp", [SEQ, 1024]); ys = dout("ys", [32, 1024]); kp = dout("kp", [SEQ, 1024]); vp = dout("vp", [SEQ, 1024])
    stp = dout("stp", [8, 128, 128]); ks = dout("ks", [32, 1024]); vs = dout("vs", [32, 1024]); sts = dout("sts", [2, 8, 128, 128])

    wbf = nc.dram_tensor("wbf", [NPIECE, 128, 4096], BF16, kind="Internal").ap()
    KTs = nc.dram_tensor("KTs", [8, 128, SEQ], BF16, kind="Internal").ap()
    Vs = nc.dram_tensor("Vs", [8, 128, SEQ // 128, 128], BF16, kind="Internal").ap()

    pieces = []
    for cg in range(8):
        pieces.append((w_in, 0, cg))
    for cg in range(2):
        pieces.append((w_ho, 0, cg))
    for cg in range(8):
        pieces.append((w_up[0], 0, cg))
    for cg in range(2):
        for kg in range(4):
            pieces.append((w_down[0], kg, cg))
    for cg in range(4):
        pieces.append((w_kv, 0, cg))
    for cg in range(2):
        pieces.append((w_dq, 0, cg))
    for cg in range(2):
        pieces.append((w_do, 0, cg))
    for cg in range(8):
        pieces.append((w_up[1], 0, cg))
    for cg in range(2):
        for kg in range(4):
            pieces.append((w_down[1], kg, cg))
    assert len(pieces) == NPIECE
    tile_seq = ([0, 1, 2, 3, 6, 7, 4, 5, 8, 9] + list(range(10, 26)) + [26, 27, 28, 29, 26, 27, 30, 31, 32, 33]
                + list(range(34, 50)))
    n_tiles_total = NTILE + 2
    use_seq = tile_seq * n_tiles_total

    with ExitStack() as st:
        def sb(name, shape, dt):
            return st.enter_context(nc.sbuf_tensor(name, list(shape), dt))

        ident_f = sb("ident_f", [128, 128], F32); ident_b = sb("ident_b", [128, 128], BF16)
        ones_f = sb("ones_f", [128, 128], F32); ones_b = sb("ones_b", [128, 128], BF16); onesD = sb("onesD", [128, 128], BF16); onesH = sb("onesH", [128, 128], BF16)
        scanm = sb("scanm", [128, 512], F32); mask2 = sb("mask2", [128, 512], F32)
        G = sb("G", [128, 80], F32); LB = sb("LB", [128, 64], F32); SG = sb("SG", [128, 4], F32)
        DL = sb("DL", [128, 512], F32); epst = sb("epst", [128, 1], F32)
        hT = sb("hT", [128, 8, 512], F32); aT = sb("aT", [128, 8, 512], BF16); aT2 = sb("aT2", [128, 8, 512], BF16)
        sqs = sb("sqs", [128, 2, 512], BF16); rstd = sb("rstd", [128, 2, 512], F32)
        mT = sb("mT", [128, 8, 512], F32)
        ost = sb("ost", [128, 4, 1024], F32)
        Vtok = sb("Vtok", [128, 4, 1024], BF16)
        R = sb("R", [128, 16, 1024], BF16)
        TH = sb("TH", [128, 2, 4, 512], F32)
        QK = sb("QK", [128, 2, 3, 512], BF16)
        ATm = sb("ATm", [128, 2, 512], BF16); Ktok = sb("Ktok", [128, 2, 512], BF16)
        Sst = sb("Sst", [128, 8, 128], F32); Sall = sb("Sall", [128, 9, 128], F32); Sbf = sb("Sbf", [128, 2, 8, 128], BF16)
        KT = sb("KT", [128, 8, 512], BF16); QT = sb("QT", [128, 8, 512], BF16)
        Wt = sb("Wt", [128, NW, 4096], BF16)
        PSQ = [st.enter_context(nc.psum_tensor(f"psq{i}", [128, 1024], F32)) for i in range(4)]

        S = Sched()
        semnames = (["pe", "act", "dve", "pool", "sp", "cst", "xld", "kts", "vss0", "vss1", "vss2", "vss3", "stl", "sto", "cvl0", "cvl1"]
                    + [f"w{i}" for i in range(NW)] + [f"ws{i}" for i in range(NW)]
                    + [f"rk{i}" for i in range(4)] + [f"rv{i}" for i in range(4)] + [f"os{i}" for i in range(4)])
        S.sems = {n: st.enter_context(nc.semaphore(n)) for n in semnames}
        block = st.enter_context(nc.Block())

        hB = [Buf(f"h{i}") for i in range(8)]; aB = [Buf(f"a{i}") for i in range(8)]; a2B = [Buf(f"a2{i}") for i in range(8)]
        mB = [Buf(f"m{i}") for i in range(8)]; sqB = [Buf("sq0"), Buf("sq1")]; rsB = [Buf("rs0"), Buf("rs1")]
        ostB = [Buf(f"ost{i}") for i in range(4)]; vtB = [Buf(f"vt{i}") for i in range(4)]
        RB = [Buf(f"R{i}") for i in range(16)]
        THB = [[Buf(f"TH{p}{i}") for i in range(4)] for p in range(2)]
        QKB = [[Buf(f"QK{p}{i}") for i in range(3)] for p in range(2)]
        ATB = [Buf("AT0"), Buf("AT1")]; KtB = [Buf("Kt0"), Buf("Kt1")]
        SB = [Buf(f"S{i}") for i in range(8)]; SallB = Buf("Sall"); SbfB = [Buf("Sbf0"), Buf("Sbf1")]
        KTB = [Buf(f"KT{i}") for i in range(8)]; QB = [Buf(f"Q{i}") for i in range(8)]
        WB = [Buf(f"W{i}") for i in range(NW)]
        BP = [Buf(f"bank{i}", excl=True) for i in range(8)]
        cB = Buf("consts")
        wbfB = [Buf(f"wbf{i}") for i in range(NPIECE)]
        ktsB = [Buf(f"kts{i}") for i in range(max(NTILE, 1))]; vsB = [Buf(f"vs{i}") for i in range(max(NTILE, 1))]
        stB = Buf("stout")

        def bank(i):
            return PSQ[i // 2][:, (i % 2) * 512:(i % 2) * 512 + 512]

        bank_rr = [0]

        def nextbank():
            b = bank_rr[0]
            bank_rr[0] = (b + 1) % 7
            return b

        def MM(out, lhsT, rhs, start, stop, reads, writes, inc=True, skip=False):
            if skip:
                S.op("pe", lambda e: e.matmul(out, lhsT, rhs, start=start, stop=stop, skip_group_check=True), reads, writes, inc)
            else:
                S.op("pe", lambda e: e.matmul(out, lhsT, rhs, start=start, stop=stop), reads, writes, inc)

        def TR(out, in_, ident, reads, writes, inc=True):
            S.op("pe", lambda e: e.transpose(out, in_, ident), reads, writes, inc)

        def ACT(out, in_, func, reads, writes, scale=1.0, bias=None):
            if bias is None:
                S.op("act", lambda e: e.activation(out=out, in_=in_, func=func, scale=scale), reads, writes)
            else:
                S.op("act", lambda e: e.activation(out=out, in_=in_, func=func, scale=scale, bias=bias), reads, writes)

        def CP(eng, out, in_, reads, writes):
            if eng == "act":
                S.op("act", lambda e: e.copy(out, in_), reads, writes)
            else:
                S.op(eng, lambda e: e.tensor_copy(out, in_), reads, writes)

        def TTo(eng, out, in0, in1, op, reads, writes):
            S.op(eng, lambda e: e.tensor_tensor(out=out, in0=in0, in1=in1, op=op), reads, writes)

        def TS(eng, out, in0, s1, s2, op0, op1, reads, writes):
            S.op(eng, lambda e: e.tensor_scalar(out=out, in0=in0, scalar1=s1, scalar2=s2, op0=op0, op1=op1), reads, writes)

        def STT(eng, out, in0, scalar, in1, op0, op1, reads, writes):
            eng = "dve"
            S.op(eng, lambda e: e.scalar_tensor_tensor(out=out, in0=in0, scalar=scalar, in1=in1, op0=op0, op1=op1), reads, writes)

        def MEMSET(eng, ap, val, writes):
            S.op(eng, lambda e: e.memset(ap, val), (), writes)

        def DMA(q, sem, out, in_, reads, writes, slow=False):
            if slow:
                return S.dma(q, sem, lambda e: e.dma_start(out=out, in_=in_, allow_slow_non_contiguous=True), reads, writes)
            return S.dma(q, sem, lambda e: e.dma_start(out=out, in_=in_), reads, writes)

        MEMSET("pool", ones_f[:], 1.0, [cB]); MEMSET("pool", ones_b[:], 1.0, [cB]); MEMSET("pool", onesD[:], 1.0 / 1024.0, [cB]); MEMSET("pool", onesH[:], 1.0 / 128.0, [cB])
        MEMSET("pool", epst[:], EPS, [cB])
        ncst = 0
        DMA("sp", "cst", ident_f[:], c_ident[:, :], [], [cB]); ncst += 1
        DMA("sp", "cst", scanm[:], c_scanm[:, :], [], [cB]); ncst += 1
        DMA("sp", "cst", mask2[:], c_mask2[:, :], [], [cB]); ncst += 1
        DMA("sp", "cst", G[:, 0:64].rearrange("p (r c) -> p r c", c=8), norm_g.rearrange("r (c p) -> p r c", p=128), [], [cB], slow=True); ncst += 1
        DMA("sp", "cst", G[:, 64:72], onorm.rearrange("(c p) -> p c", p=128), [], [cB], slow=True); ncst += 1
        DMA("sp", "cst", G[:, 72:80], kvg.rearrange("(c p) -> p c", p=128), [], [cB], slow=True); ncst += 1
        DMA("sp", "cst", LB[:, 0:16].rearrange("p (l c) -> p l c", c=8), lbl.rearrange("l (c p) -> p l c", p=128), [], [cB], slow=True); ncst += 1
        DMA("sp", "cst", SG[:, 0:1], subg.rearrange("(p o) -> p o", o=1), [], [cB], slow=True); ncst += 1
        DMA("sp", "cst", DL[:, 0:256], bass.AP(dlam.tensor, 0, [[0, 128], [1, 256]]), [], [cB], slow=True); ncst += 1
        cB.w = ("cst", 16 * ncst)
        cB.r = []
        CP("dve", ident_b[:], ident_f[:], [cB], [cB])
        ACT(LB[:, 0:16], LB[:, 0:16], AF.Exp, [cB], [cB])
        TTo("dve", LB[:, 16:24], LB[:, 0:8], LB[:, 8:16], ALU.add, [cB], [cB])
        S.op("dve", lambda e: e.reciprocal(LB[:, 16:24], LB[:, 16:24]), [cB], [cB])
        TTo("dve", LB[:, 24:32], LB[:, 0:8], LB[:, 16:24], ALU.mult, [cB], [cB])
        TTo("dve", LB[:, 32:40], LB[:, 8:16], LB[:, 16:24], ALU.mult, [cB], [cB])
        S.op("dve", lambda e: e.tensor_scalar_mul(LB[:, 40:48], LB[:, 32:40], -1.0), [cB], [cB])
        TTo("dve", DL[:, 256:320], DL[:, 0:64], DL[:, 64:128], ALU.mult, [cB], [cB])
        TTo("dve", DL[:, 320:384], DL[:, 128:192], DL[:, 192:256], ALU.mult, [cB], [cB])
        S.op("dve", lambda e: e.reduce_sum(DL[:, 384:386], DL[:, 256:384].rearrange("p (a b) -> p a b", b=64), AX.X), [cB], [cB])
        ACT(DL[:, 384:386], DL[:, 384:386], AF.Exp, [cB], [cB])
        TTo("dve", SG[:, 1:2], DL[:, 385:386], DL[:, 384:385], ALU.subtract, [cB], [cB])
        S.op("dve", lambda e: e.tensor_scalar_add(SG[:, 1:2], SG[:, 1:2], -LAM_INIT), [cB], [cB])
        S.op("dve", lambda e: e.tensor_scalar_mul(SG[:, 2:3], SG[:, 0:1], 1.0 - LAM_INIT), [cB], [cB])
        lb_ap = lambda h: LB[:, 24 + h:25 + h]
        oml_ap = lambda h: LB[:, 32 + h:33 + h]
        noml_ap = lambda h: LB[:, 40 + h:41 + h]
        neg_lam = SG[:, 1:2]; subgs = SG[:, 2:3]

        def gcol(r, c):
            return G[:, r * 8 + c:r * 8 + c + 1]

        stg = [hT[:, :, :], mT[:, :, :]]
        stgB = [hB, mB]
        cast_eng = ["dve", "act", "pool"]
        for i, (wap, kg, cg) in enumerate(pieces):
            sI = i % 2
            src = wap[kg * 1024:(kg + 1) * 1024, cg * 512:(cg + 1) * 512].rearrange("(kc p) j -> p kc j", p=128)
            DMA("sp", f"cvl{sI}", stg[sI], src, [], stgB[sI])
            slot = i % NW
            wv = Wt[:, slot, :].rearrange("p (kc j) -> p kc j", j=512)
            CP(cast_eng[i % 3], wv, stg[sI], stgB[sI], [WB[slot]])
            DMA("pool", f"ws{slot}", wbf[i], Wt[:, slot, :], [WB[slot]], [wbfB[i]])

        ring = {"issued": 0, "cur": 0}

        def ring_issue(i):
            slot = i % NW
            DMA("sp", f"w{slot}", Wt[:, slot, :], wbf[use_seq[i]], [wbfB[use_seq[i]]], [WB[slot]])

        def next_piece(expect):
            while ring["issued"] < min(len(use_seq), ring["cur"] + NW):
                ring_issue(ring["issued"])
                ring["issued"] += 1
            i = ring["cur"]
            assert use_seq[i] == expect, (i, use_seq[i], expect)
            ring["cur"] += 1
            slot = i % NW
            return Wt[:, slot, :].rearrange("p (kc j) -> p kc j", j=512), WB[slot]

        def proj_fm(inT, inB, N, piece_ids, evac):
            for pi, pid in enumerate(piece_ids):
                wap, wB = next_piece(pid)
                for o in range(4):
                    b = nextbank()
                    for kc in range(8):
                        MM(bank(b)[:, :N], wap[:, kc, o * 128:(o + 1) * 128], inT[:, kc, :N], kc == 0, kc == 7,
                           [wB, inB[kc]], [BP[b]], inc=(kc == 7))
                    evac(pi * 4 + o, bank(b)[:, :N], BP[b])

        def proj_down(uT, uB, N, piece_base, evac):
            for cg in range(2):
                bs = [nextbank() for _ in range(4)]
                for kg in range(4):
                    wap, wB = next_piece(piece_base + cg * 4 + kg)
                    for o in range(4):
                        for kc in range(8):
                            j = kg * 8 + kc
                            MM(bank(bs[o])[:, :N], wap[:, kc, o * 128:(o + 1) * 128], uT(j)[:, :N],
                               (kg == 0 and kc == 0), (kg == 3 and kc == 7), [wB, uB(j)], [BP[bs[o]]],
                               inc=(kc == 7))
                for o in range(4):
                    evac(cg * 4 + o, bank(bs[o])[:, :N], BP[bs[o]])

        def proj_tm(inT, inB, N, pid, evac):
            wap, wB = next_piece(pid)
            nsub = (N + 127) // 128
            subr = min(N, 128)
            for s in range(nsub):
                b = nextbank()
                for kc in range(8):
                    MM(bank(b)[:subr, :], inT[:, kc, s * 128:s * 128 + subr], wap[:, kc, :], kc == 0, kc == 7,
                       [wB, inB[kc]], [BP[b]], inc=(kc == 7))
                evac(s, bank(b)[:subr, :], BP[b])

        def norm_rstd(chunks, ones_ap, N, ri, psum_src=True):
            b = nextbank()
            n = len(chunks)
            for c, (ap, Bs) in enumerate(chunks):
                sl = c % 2
                if psum_src or c % 2 == 0:
                    ACT(sqs[:, sl, :N], ap, AF.Square, Bs, [sqB[sl]])
                else:
                    TTo("pool", sqs[:, sl, :N], ap, ap, ALU.mult, Bs, [sqB[sl]])
                MM(bank(b)[:, :N], ones_ap[:, :], sqs[:, sl, :N], c == 0, c == n - 1, [sqB[sl], cB], [BP[b]], inc=True)
            ACT(rstd[:, ri, :N], bank(b)[:, :N], AF.Ln, [BP[b], cB], [rsB[ri]], bias=epst[:, 0:1])
            ACT(rstd[:, ri, :N], rstd[:, ri, :N], AF.Exp, [rsB[ri]], [rsB[ri]], scale=-0.5)

        def h_chunks(N):
            return [(hT[:, c, :N], [hB[c]]) for c in range(8)]

        def m_chunks(N):
            return [(mT[:, c, :N], [mB[c]]) for c in range(8)]

        def apply_norm(dstT, dstB, grow, N, ri):
            for c in range(8):
                eng = "dve" if c % 2 == 0 else "pool"
                STT(eng, dstT[:, c, :N], hT[:, c, :N], gcol(grow, c), rstd[:, ri, :N], ALU.mult, ALU.mult,
                    [hB[c], rsB[ri], cB], [dstB[c]])

        def resid_add(grow, N):
            ACT(rstd[:, 0, :N], bank(7)[:, :N], AF.Ln, [BP[7], cB], [rsB[0]], bias=epst[:, 0:1])
            ACT(rstd[:, 0, :N], rstd[:, 0, :N], AF.Exp, [rsB[0]], [rsB[0]], scale=-0.5)
            for c in range(8):
                eng = "dve" if c % 2 == 0 else "pool"
                TTo(eng, mT[:, c, :N], mT[:, c, :N], rstd[:, 0, :N], ALU.mult, [mB[c], rsB[0]], [mB[c]])
                STT(eng, hT[:, c, :N], mT[:, c, :N], gcol(grow, c), hT[:, c, :N], ALU.mult, ALU.add,
                    [mB[c], hB[c], cB], [hB[c]])

        def stat_mm(idx, N):
            sl = idx % 2
            MM(bank(7)[:, :N], onesD[:, :], sqs[:, sl, :N], idx == 0, idx == 7, [sqB[sl], cB], [BP[7]], inc=True)

        def evac_to_m(idx, ps, pB, N):
            ACT(sqs[:, idx % 2, :N], ps, AF.Square, [pB], [sqB[idx % 2]])
            if idx >= 1:
                stat_mm(idx - 1, N)
            if idx == 7:
                stat_mm(7, N)
            if idx % 2 == 0:
                CP("act", mT[:, idx, :N], ps, [pB], [mB[idx]])
            else:
                CP("dve", mT[:, idx, :N], ps, [pB], [mB[idx]])

        def Rf32(slot):
            return R[:, slot, :].bitcast(F32)

        def mlp(layer, N, up_base, down_base):
            norm_rstd(h_chunks(N), onesD, N, 0)
            apply_norm(aT, aB, layer * 4 + 2, N, 0)

            def ev_up(j, ps, pB):
                slot = j // 2
                half = (j % 2) * 512
                tmp = TH[:, j % 2, 0, :].bitcast(BF16)[:, 0:512]
                ACT(tmp[:, :N], ps, AF.Relu, [pB], [THB[j % 2][0]])
                TTo("pool" if j % 2 == 0 else "dve", R[:, slot, half:half + N], tmp[:, :N], tmp[:, :N], ALU.mult,
                    [THB[j % 2][0]], [RB[slot]])
            proj_fm(aT, aB, N, list(range(up_base, up_base + 8)), ev_up)
            proj_down(lambda j: R[:, j // 2, (j % 2) * 512:(j % 2) * 512 + 512], lambda j: RB[j // 2], N, down_base,
                      lambda idx, ps, pB: evac_to_m(idx, ps, pB, N))
            resid_add(layer * 4 + 3, N)

        def load_x(src_rows, N):
            nsub = (N + 127) // 128
            subr = min(N, 128)
            xv = mT[:, :, :].rearrange("p a b -> p (a b)").rearrange("p (s f) -> p s f", f=1024)
            if N >= 128:
                DMA("sp", "xld", xv[:, :nsub, :], src_rows.rearrange("(s p) f -> p s f", p=128), [], mB)
            else:
                DMA("sp", "xld", xv[:subr, 0, :], src_rows, [], mB)
            for c in range(8):
                b = nextbank()
                for s in range(nsub):
                    TR(bank(b)[:, s * 128:s * 128 + subr], xv[:subr, s, c * 128:(c + 1) * 128], ident_f[:subr, :subr],
                       mB + [cB], [BP[b]], inc=(s == nsub - 1))
                CP("act" if c % 2 == 0 else "dve", hT[:, c, :N], bank(b)[:, :N], [BP[b]], [hB[c]])

        def store_rows_from_fm(srcT, srcB, N, dst_rows):
            nsub = (N + 127) // 128
            subr = min(N, 128)
            for s in range(nsub):
                slot = ost_rr[0]
                ost_rr[0] = (slot + 1) % 4
                for half in range(2):
                    b = nextbank()
                    for cc in range(4):
                        c = half * 4 + cc
                        TR(bank(b)[:subr, cc * 128:(cc + 1) * 128], srcT[:, c, s * 128:s * 128 + subr], ident_f[:, :],
                           [srcB[c], cB], [BP[b]], inc=(cc == 3))
                    CP("act" if half == 0 else "dve", ost[:subr, slot, half * 512:(half + 1) * 512], bank(b)[:subr, :],
                       [BP[b]], [ostB[slot]])
                DMA("pool", f"os{slot}", dst_rows[s * 128:s * 128 + subr, :], ost[:subr, slot, :], [ostB[slot]], [])

        ost_rr = [0]

        def hgrn(N, CH):
            npair = (N + 127) // 128
            pw = min(N, 128)
            nch = N // CH
            norm_rstd(h_chunks(N), onesD, N, 0)
            apply_norm(aT, aB, 0, N, 0)
            def ev_q(j, ps, pB):
                ACT(R[:, 8 + j // 2, (j % 2) * 512:(j % 2) * 512 + N], ps, AF.Silu, [pB], [RB[8 + j // 2]])
            proj_fm(aT, aB, N, [0, 1], ev_q)
            def ev_f(j, ps, pB):
                ACT(Rf32(j)[:, :N], ps, AF.Sigmoid, [pB], [RB[j]])
            proj_fm(aT, aB, N, [2, 3], ev_f)
            def ev_g(j, ps, pB):
                ACT(R[:, 12 + j // 2, (j % 2) * 512:(j % 2) * 512 + N], ps, AF.Silu, [pB], [RB[12 + j // 2]])
            proj_fm(aT, aB, N, [6, 7], ev_g)
            subr = min(N, 128)
            for half in range(2):
                def ev_v(s, ps, pB, half=half):
                    CP("dve" if s % 2 == 0 else "act", Vtok[:subr, s, half * 512:(half + 1) * 512], ps, [pB], [vtB[s]])
                proj_tm(aT, aB, N, 4 + half, ev_v)

            def stageA(h):
                pb = h % 2
                T = lambda i: TH[:, pb, i, :N]
                TB = THB[pb]
                sig = Rf32(h)[:, :N]
                sq_ = R[:, 8 + h // 2, (h % 2) * 512:(h % 2) * 512 + N]
                TS("dve", T(0), sig, oml_ap(h), lb_ap(h), ALU.mult, ALU.add, [RB[h], cB], [TB[0]])
                TS("pool", T(1), sig, noml_ap(h), oml_ap(h), ALU.mult, ALU.add, [RB[h], cB], [TB[1]])
                ACT(T(0), T(0), AF.Ln, [TB[0]], [TB[0]])
                S.op("dve", lambda e: e.tensor_tensor_scan(out=T(2), data0=scanm[:, :N], data1=T(0), initial=0.0,
                                                           op0=ALU.mult, op1=ALU.add), [TB[0], cB], [TB[2]])
                ACT(T(0), T(2), AF.Exp, [TB[2]], [TB[0]])
                ACT(T(3), T(2), AF.Exp, [TB[2]], [TB[3]], scale=-1.0)
                Qt = QK[:, pb, 0, :N]; Kt = QK[:, pb, 1, :N]; Kh = QK[:, pb, 2, :N]
                TTo("dve", Qt, sq_, T(0), ALU.mult, [RB[8 + h // 2], TB[0]], [QKB[pb][0]])
                TTo("pool", Kt, T(1), T(3), ALU.mult, [TB[1], TB[3]], [QKB[pb][1]])
                Ev = TH[:, pb, 0, :N].rearrange("p (c t) -> p c t", t=CH)
                TTo("pool", Kh.rearrange("p (c t) -> p c t", t=CH), Kt.rearrange("p (c t) -> p c t", t=CH),
                    Ev[:, :, CH - 1:CH].broadcast_to([128, nch, CH]), ALU.mult, [QKB[pb][1], TB[0]], [QKB[pb][2]])

            def stageA2(h):
                pb = h % 2
                Qt = QK[:, pb, 0, :N]; Kt = QK[:, pb, 1, :N]; Kh = QK[:, pb, 2, :N]
                for j in range(npair):
                    MM(bank(0)[:pw, j * 128:j * 128 + pw], Kt[:, j * 128:j * 128 + pw], Qt[:, j * 128:j * 128 + pw], True, True,
                       [QKB[pb][0], QKB[pb][1]], [BP[0]], inc=(j == npair - 1))
                TTo("dve", ATm[:pw, pb, :N], bank(0)[:pw, :N], mask2[:pw, :N], ALU.mult, [BP[0], cB], [ATB[pb]])
                pT = bank(1).bitcast(BF16)
                for j in range(npair):
                    TR(pT[:pw, j * 128:(j + 1) * 128], Kh[:, j * 128:j * 128 + pw], ident_b[:, :], [QKB[pb][2], cB], [BP[1]],
                       inc=(j == npair - 1))
                CP("act", Ktok[:pw, pb, :npair * 128], pT[:pw, :npair * 128], [BP[1]], [KtB[pb]])

            def stageA3(h):
                pb = h % 2
                TB = THB[pb]
                dsl = lambda c: PSQ[1][:, (c % 2) * 512 + (c // 2) * 128:(c % 2) * 512 + (c // 2) * 128 + 128]
                for c in range(nch):
                    j = (c * CH) // 128
                    r0 = (c * CH) % 128
                    bb = 2 + c % 2
                    MM(dsl(c), Ktok[r0:r0 + CH, pb, j * 128:(j + 1) * 128],
                       Vtok[r0:r0 + CH, j, h * 128:(h + 1) * 128], True, True, [KtB[pb], vtB[j]], [BP[bb]],
                       inc=(c >= nch - 2))
                CP("pool", Sall[:, 0, :], Sst[:, h, :], [SB[h]], [SallB])
                for c in range(nch):
                    bb = 2 + c % 2
                    STT("dve", Sall[:, c + 1, :], Sall[:, c, :], TH[:, pb, 0, c * CH + CH - 1:c * CH + CH],
                        dsl(c), ALU.mult, ALU.add, [SallB, TB[0], BP[bb]], [SallB])
                CP("pool", Sst[:, h, :], Sall[:, nch, :], [SallB], [SB[h]])
                CP("act", Sbf[:, pb, :nch, :], Sall[:, 0:nch, :], [SallB], [SbfB[pb]])

            def stageB(h):
                pb = h % 2
                Qt = QK[:, pb, 0, :N]
                for j in range(npair):
                    MM(bank(4)[:, j * 128:j * 128 + pw], Vtok[:pw, j, h * 128:(h + 1) * 128], ATm[:pw, pb, j * 128:j * 128 + pw],
                       j == 0, False, [vtB[j], ATB[pb]], [BP[4]], inc=False, skip=True)
                for c in range(nch):
                    MM(bank(4)[:, c * CH:(c + 1) * CH], Sbf[:, pb, c, :], Qt[:, c * CH:(c + 1) * CH], False, c == nch - 1,
                       [SbfB[pb], QKB[pb][0]], [BP[4]], inc=(c == nch - 1), skip=True)
                ACT(sqs[:, 0, :N], bank(4)[:, :N], AF.Square, [BP[4]], [sqB[0]])
                MM(bank(5)[:, :N], onesH[:, :], sqs[:, 0, :N], True, True, [sqB[0], cB], [BP[5]])
                ACT(rstd[:, 1, :N], bank(5)[:, :N], AF.Ln, [BP[5], cB], [rsB[1]], bias=epst[:, 0:1])
                ACT(rstd[:, 1, :N], rstd[:, 1, :N], AF.Exp, [rsB[1]], [rsB[1]], scale=-0.5)

            def stageB2(h):
                pb = h % 2
                tmp = TH[:, pb, 2, :N]
                STT("dve", tmp, bank(4)[:, :N], G[:, 64 + h:65 + h], rstd[:, 1, :N], ALU.mult, ALU.mult,
                    [BP[4], rsB[1], cB], [THB[pb][2]])
                sg_ = R[:, 12 + h // 2, (h % 2) * 512:(h % 2) * 512 + N]
                TTo("pool", aT2[:, h, :N], tmp, sg_, ALU.mult, [THB[pb][2], RB[12 + h // 2]], [a2B[h]])

            stageA(0); stageA2(0)
            for h in range(8):
                if h + 1 < 8:
                    stageA(h + 1)
                if h >= 1:
                    stageB2(h - 1)
                stageA3(h)
                if h + 1 < 8:
                    stageA2(h + 1)
                stageB(h)
            stageB2(7)
            proj_fm(aT2, a2B, N, [8, 9], lambda idx, ps, pB: evac_to_m(idx, ps, pB, N))
            resid_add(1, N)

        def attend(h, N, kblocks):
            nkb = len(kblocks)

            Pbufs = [(TH[:, 0, 0, :].bitcast(BF16), [THB[0][0]]), (TH[:, 1, 0, :].bitcast(BF16), [THB[1][0]]),
                     (QK[:, 0, 0:2, :].rearrange("p a b -> p (a b)"), [QKB[0][0], QKB[0][1]])]

            def qk(i):
                kb = kblocks[i]
                par = i % 2
                q = PSQ[par]
                cl = kb["cl"]; nk = kb["nk"]
                P, PBl = Pbufs[i % 3]
                MM(q[:nk, cl:N], kb["KT"][0:64, :nk], QT[0:64, h, cl:N], True, True, kb["rk"] + [QB[h]], [BP[2 * par]], inc=False)
                MM(q[:nk, 512 + cl:512 + N], kb["KT"][64:128, :nk], QT[64:128, h, cl:N], True, True, kb["rk"] + [QB[h]],
                   [BP[2 * par + 1]], inc=True)
                qv = q[:nk, :].rearrange("p (two n) -> p two n", two=2)[:, :, cl:N]
                pv = P[:nk, :].rearrange("p (two n) -> p two n", two=2)[:, :, cl:N]
                ACT(pv, qv, AF.Exp, [BP[2 * par], BP[2 * par + 1]], PBl, scale=0.125)
                if kb["corner"]:
                    cv_ = P[64:128, :].rearrange("p (two n) -> p two n", two=2)[:, :, cl:cl + 64]
                    MEMSET("pool", cv_, 0.0, PBl)

            acc2 = TH[:, 1, 3, :]

            def pv(i):
                kb = kblocks[i]
                cl = kb["cl"]; nk = kb["nk"]
                P, PBl = Pbufs[i % 3]
                first = (i == 0); last = (i == nkb - 1)
                MM(bank(4)[:, cl:N], kb["V"][:nk, :], P[:nk, cl:N], first, last, kb["rv"] + PBl, [BP[4]], inc=False)
                MM(bank(5)[:, cl:N], kb["V"][:nk, :], P[:nk, 512 + cl:512 + N], first, last, kb["rv"] + PBl, [BP[5]], inc=False)
                MM(bank(6)[:, cl:N], ones_b[:nk, :], P[:nk, cl:N], first, last, PBl + [cB], [BP[6]], inc=True)
                if first:
                    CP("dve", acc2[:nk, cl:N], P[:nk, 512 + cl:512 + N], PBl, [THB[1][3]])
                else:
                    TTo("dve", acc2[:nk, cl:N], acc2[:nk, cl:N], P[:nk, 512 + cl:512 + N], ALU.add, PBl + [THB[1][3]], [THB[1][3]])
                if last:
                    MM(bank(7)[:, :N], ones_f[:, :], acc2[:, :N], True, True, [THB[1][3], cB], [BP[7]], inc=True)

            qk(0)
            if nkb > 1:
                qk(1)
            for i in range(nkb):
                if i + 2 < nkb:
                    qk(i + 2)
                pv(i)
            r1 = TH[:, 0, 1, :N]; r2 = TH[:, 0, 2, :N]; o1 = TH[:, 1, 1, :N]; o2 = TH[:, 1, 2, :N]
            ACT(r1, bank(6)[:, :N], AF.Ln, [BP[6]], [THB[0][1]])
            ACT(r2, bank(7)[:, :N], AF.Ln, [BP[7]], [THB[0][2]])
            CP("dve", o1, bank(4)[:, :N], [BP[4]], [THB[1][1]])
            CP("dve", o2, bank(5)[:, :N], [BP[5]], [THB[1][2]])
            ACT(r1, r1, AF.Exp, [THB[0][1]], [THB[0][1]], scale=-1.0)
            ACT(r2, r2, AF.Exp, [THB[0][2]], [THB[0][2]], scale=-1.0)
            TTo("dve", o1, o1, r1, ALU.mult, [THB[1][1], THB[0][1]], [THB[1][1]])
            TTo("dve", o2, o2, r2, ALU.mult, [THB[1][2], THB[0][2]], [THB[1][2]])
            STT("dve", mT[:, h, :N], o2, neg_lam, o1, ALU.mult, ALU.add, [THB[1][1], THB[1][2], cB], [mB[h]])

        def attn_subln(N):
            for h in range(8):
                ACT(sqs[:, h % 2, :N], mT[:, h, :N], AF.Square, [mB[h]], [sqB[h % 2]])
                MM(bank(h)[:, :N], onesH[:, :], sqs[:, h % 2, :N], True, True, [sqB[h % 2], cB], [BP[h]])
            for h in range(8):
                rr = TH[:, h % 2, 3, :N]
                ACT(rr, bank(h)[:, :N], AF.Ln, [BP[h], cB], [THB[h % 2][3]], bias=epst[:, 0:1])
                ACT(rr, rr, AF.Exp, [THB[h % 2][3]], [THB[h % 2][3]], scale=-0.5)
                STT("dve", aT[:, h, :N], mT[:, h, :N], subgs, rr, ALU.mult, ALU.mult, [mB[h], THB[h % 2][3], cB], [aB[h]])

        def diff_layer(N, mode, t=0, j=0):
            nsub = (N + 127) // 128
            subr = min(N, 128)
            norm_rstd(h_chunks(N), onesD, N, 0)
            apply_norm(aT, aB, 4, N, 0)
            apply_norm(aT2, a2B, 9, N, 0)
            kdst = kp[t * TT:(t + 1) * TT, :] if mode == "prompt" else ks[j * 16:(j + 1) * 16, :]
            vdst = vp[t * TT:(t + 1) * TT, :] if mode == "prompt" else vs[j * 16:(j + 1) * 16, :]
            for which, dst in ((0, kdst), (1, vdst)):
                for half in range(2):
                    def ev_kv(s, ps, pB, which=which, half=half):
                        eng = "act" if (s + half) % 2 == 0 else "dve"
                        CP(eng, ost[:subr, s, half * 512:(half + 1) * 512], ps, [pB], [ostB[s]])
                        if which == 1:
                            CP("dve" if eng == "act" else "act", Vtok[:subr, s, half * 512:(half + 1) * 512],
                               ost[:subr, s, half * 512:(half + 1) * 512], [ostB[s]], [vtB[s]])
                    proj_tm(aT2, a2B, N, 26 + which * 2 + half, ev_kv)
                for s in range(nsub):
                    DMA("pool", f"os{s}", dst[s * 128:s * 128 + subr, :], ost[:subr, s, :], [ostB[s]], [])
            if KCUT2 < 2:
                ring["cur"] += 6; ring["issued"] = max(ring["issued"], ring["cur"])
                return
            def ev_k(c, ps, pB):
                CP("act" if c % 2 == 0 else "dve", KT[:, c, :N], ps, [pB], [KTB[c]])
            proj_fm(aT2, a2B, N, [26, 27], ev_k)
            def ev_qq(c, ps, pB):
                CP("dve" if c % 2 == 0 else "act", QT[:, c, :N], ps, [pB], [QB[c]])
            proj_fm(aT, aB, N, [30, 31], ev_qq)
            if mode == "prompt":
                if t < NTILE - 1:
                    DMA("pool", "kts", KTs[:, :, t * TT:(t + 1) * TT].rearrange("h p n -> p h n"), KT[:, :, :], KTB, [ktsB[t]], slow=True)
                    for s in range(4):
                        DMA("pool", f"vss{s}", Vs[:, :, t * 4 + s, :].rearrange("h p e -> p h e"),
                            Vtok[:, s, :].rearrange("p (h e) -> p h e", h=8), [vtB[s]], [vsB[t]], slow=True)
                npast = t * TT
                nblk = (npast + 2047) // 2048
                for h in range(8):
                    kbs = []
                    for blk in range(nblk):
                        rs = (h * nblk + blk) % 4
                        k0 = blk * 2048
                        k1 = min(npast, k0 + 2048)
                        nkb_ = (k1 - k0) // 128
                        Kr = R[:, 4 * rs:4 * rs + 2, :].rearrange("p a b -> p (a b)")
                        Vr = R[:, 4 * rs + 2:4 * rs + 4, :].rearrange("p a b -> p (a b)").rearrange("p (k e) -> p k e", e=128)
                        tl = list(range(k0 // TT, (k1 + TT - 1) // TT))
                        DMA("sp", f"rk{rs}", Kr[:, :k1 - k0], KTs[h, :, k0:k1], [ktsB[x] for x in tl], [RB[4 * rs], RB[4 * rs + 1]])
                        DMA("sp", f"rv{rs}", Vr[:, :nkb_, :], Vs[h, :, k0 // 128:k1 // 128, :], [vsB[x] for x in tl],
                            [RB[4 * rs + 2], RB[4 * rs + 3]])
                        for kb_ in range(nkb_):
                            kbs.append(dict(KT=Kr[:, kb_ * 128:(kb_ + 1) * 128], V=Vr[:, kb_, :], nk=128, cl=0, corner=False,
                                            rk=[RB[4 * rs], RB[4 * rs + 1]], rv=[RB[4 * rs + 2], RB[4 * rs + 3]]))
                    for jj in range(4):
                        kbs.append(dict(KT=KT[:, h, jj * 128:(jj + 1) * 128], V=Vtok[:, jj, h * 128:(h + 1) * 128], nk=128,
                                        cl=128 * jj, corner=True, rk=[KTB[h]], rv=[vtB[jj]]))
                    attend(h, N, kbs)
                attn_subln(N)
            else:
                xv = mT[:, :, :].rearrange("p a b -> p (a b)").rearrange("p (s f) -> p s f", f=1024)
                for half in range(2):
                    DMA("sp", "xld", xv, ck[j, half * 512:(half + 1) * 512, :].rearrange("(s p) f -> p s f", p=128), [], mB)
                    for c in range(8):
                        b = nextbank()
                        for s in range(4):
                            TR(bank(b)[:, s * 128:(s + 1) * 128], xv[:, s, c * 128:(c + 1) * 128], ident_f[:, :], mB + [cB], [BP[b]],
                               inc=(s == 3))
                        CP("act" if c % 2 == 0 else "dve", R[:, c, half * 512:(half + 1) * 512], bank(b), [BP[b]], [RB[c]])
                for half in range(2):
                    DMA("sp", "xld", xv, cv[j, half * 512:(half + 1) * 512, :].rearrange("(s p) f -> p s f", p=128), [], mB)
                    for s in range(4):
                        CP("dve" if s % 2 == 0 else "pool", R[:, 8 + half * 4 + s, :], xv[:, s, :], mB, [RB[8 + half * 4 + s]])
                for h in range(8):
                    kbs = []
                    for kb_ in range(8):
                        kbs.append(dict(KT=R[:, h, kb_ * 128:(kb_ + 1) * 128], V=R[:, 8 + kb_, h * 128:(h + 1) * 128], nk=128, cl=0,
                                        corner=False, rk=[RB[h]], rv=[RB[8 + kb_]]))
                    kbs.append(dict(KT=KT[:, h, 0:N], V=Vtok[:N, 0, h * 128:(h + 1) * 128], nk=N, cl=0, corner=False,
                                    rk=[KTB[h]], rv=[vtB[0]]))
                    attend(h, N, kbs)
                attn_subln(N)
            proj_fm(aT, aB, N, [32, 33], lambda idx, ps, pB: evac_to_m(idx, ps, pB, N))
            resid_add(5, N)

        for h in range(8):
            MEMSET("pool", Sst[:, h, :], 0.0, [SB[h]])
        for t in range(NTILE):
            if STOP < 2:
                break
            load_x(xp[t * TT:(t + 1) * TT, :], TT)
            if STOP < 3:
                store_rows_from_fm(hT, hB, TT, yp[t * TT:(t + 1) * TT, :])
                continue
            hgrn(TT, 64)
            if STOP < 4:
                ring["cur"] += 40; ring["issued"] = max(ring["issued"], ring["cur"])
                store_rows_from_fm(hT, hB, TT, yp[t * TT:(t + 1) * TT, :])
                continue
            mlp(0, TT, 10, 18)
            if STOP < 5:
                ring["cur"] += 24; ring["issued"] = max(ring["issued"], ring["cur"])
                store_rows_from_fm(hT, hB, TT, yp[t * TT:(t + 1) * TT, :])
                continue
            diff_layer(TT, "prompt", t=t)
            if KCUT2 < 2:
                ring["cur"] += 16; ring["issued"] = max(ring["issued"], ring["cur"])
                store_rows_from_fm(hT, hB, TT, yp[t * TT:(t + 1) * TT, :])
                continue
            mlp(1, TT, 34, 42)
            store_rows_from_fm(hT, hB, TT, yp[t * TT:(t + 1) * TT, :])
        evs = [DMA("pool", "sto", stp.rearrange("h k v -> k h v"), Sst[:, :, :], SB, [stB], slow=True)]
        for j in range(2):
            if STOP < 6:
                break
            DMA("sp", "stl", Sst[:, :, :], sth[j].rearrange("h k v -> k h v"), [stB], SB, slow=True)
            load_x(xs[j * 16:(j + 1) * 16, :], 16)
            hgrn(16, 16)
            mlp(0, 16, 10, 18)
            diff_layer(16, "sample", j=j)
            mlp(1, 16, 34, 42)
            store_rows_from_fm(hT, hB, 16, ys[j * 16:(j + 1) * 16, :])
            evs.append(DMA("pool", "sto", sts[j].rearrange("h k v -> k h v"), Sst[:, :, :], SB, [stB], slow=True))
        for k in ["sto", "os0", "os1", "os2", "os3", "kts", "vss0", "vss1", "vss2", "vss3"]:
            if k in S.dma_cnt:
                S.wait_event("pool", (k, S.dma_cnt[k]))
        S.replay(block)
    return nc, S


_CONSTS = None


def _consts():
    global _CONSTS
    if _CONSTS is None:
        ident = np.eye(128, dtype=np.float32)
        scanm = np.ones((128, 512), np.float32)
        scanm[:, ::64] = 0.0
        s = np.arange(128)[:, None]
        t = np.arange(128)[None, :]
        m = ((s // 64 == t // 64) & (s <= t)).astype(np.float32)
        mask2 = np.tile(m, (1, 4))
        _CONSTS = (ident, scanm, mask2)
    return _CONSTS


_NC_CACHE = {}


def kernel(x_prompt, x_sample, cache_k, cache_v, state_hgrn, norm_g, w_hgrn_in, hgrn_lb_logits, hgrn_onorm_g, w_hgrn_out,
           kv_norm_g, w_kv, w_dq, diff_lambda, diff_subln_g, w_do, w_up, w_down):
    f = lambda a: np.ascontiguousarray(np.asarray(a), dtype=np.float32)
    x_prompt = f(x_prompt); x_sample = f(x_sample); cache_k = f(cache_k); cache_v = f(cache_v); state_hgrn = f(state_hgrn)
    B, SEQ, D = x_prompt.shape
    assert B == 8 and D == 1024 and SEQ % TT == 0
    if SEQ not in _NC_CACHE:
        _NC_CACHE[SEQ] = build_nc(SEQ)[0]
    nc = _NC_CACHE[SEQ]
    ident, scanm, mask2 = _consts()
    shared = dict(
        norm_g=f(norm_g).reshape(8, 1024), w_in=f(w_hgrn_in)[0], lbl=f(hgrn_lb_logits), onorm=f(hgrn_onorm_g)[0],
        w_ho=f(w_hgrn_out)[0], kvg=f(kv_norm_g), w_kv=f(w_kv), w_dq=f(w_dq)[0], dlam=f(diff_lambda).reshape(256),
        subg=f(diff_subln_g).reshape(128), w_do=f(w_do)[0], w_up=f(w_up), w_down=f(w_down),
        c_ident=ident, c_scanm=scanm, c_mask2=mask2)
    in_maps = []
    for c in range(8):
        m = dict(shared)
        m["xp"] = x_prompt[c]
        m["xs"] = x_sample[2 * c:2 * c + 2].reshape(32, 1024)
        m["ck"] = cache_k[2 * c:2 * c + 2].reshape(2, 1024, 1024)
        m["cv"] = cache_v[2 * c:2 * c + 2].reshape(2, 1024, 1024)
        m["sth"] = state_hgrn[0, 2 * c:2 * c + 2]
        in_maps.append(m)
    res = run_bass_kernel_spmd(nc, in_maps, core_ids=list(range(8)))
    r = res.results
    y_prompt = np.stack([r[c]["yp"] for c in range(8)]).reshape(8, SEQ, 1024)
    y_sample = np.concatenate([r[c]["ys"].reshape(2, 16, 1024) for c in range(8)])
    k_prompt = np.stack([r[c]["kp"] for c in range(8)]).reshape(8, SEQ, 8, 2, 64)
    v_prompt = np.stack([r[c]["vp"] for c in range(8)]).reshape(8, SEQ, 8, 128)
    st_prompt = np.stack([r[c]["stp"] for c in range(8)])[None]
    k_sample = np.concatenate([r[c]["ks"].reshape(2, 16, 8, 2, 64) for c in range(8)])
    v_sample = np.concatenate([r[c]["vs"].reshape(2, 16, 8, 128) for c in range(8)])
    st_sample = np.concatenate([r[c]["sts"] for c in range(8)])[None]
    return (y_prompt.astype(np.float32), y_sample.astype(np.float32), k_prompt.astype(np.float32), v_prompt.astype(np.float32),
            st_prompt.astype(np.float32), k_sample.astype(np.float32), v_sample.astype(np.float32), st_sample.astype(np.float32))
```

```python
import math
import os
KDBG = 'nopool'
KCUT = int(os.environ.get('KCUT', '99'))
KCUT2 = int(os.environ.get('KCUT2', '99'))
from contextlib import ExitStack
import numpy as np
import concourse.bass as bass
import concourse.mybir as mybir
from concourse.bass_utils import run_bass_kernel_spmd

F32 = mybir.dt.float32
BF16 = mybir.dt.bfloat16
AF = mybir.ActivationFunctionType
ALU = mybir.AluOpType
AX = mybir.AxisListType

TT = 512
EPS = 1e-6
LAM_INIT = 0.8 - 0.6 * math.exp(-0.3 * 1)
NW = 4
NPIECE = 50
STRICT = True


class Buf:
    __slots__ = ("name", "w", "r", "excl")

    def __init__(self, name, excl=False):
        self.name = name
        self.w = None
        self.r = []
        self.excl = excl


class Eng:
    def __init__(self, name, in_order_safe=False):
        self.name = name
        self.ops = []
        self.sem = name
        self.count = 0
        self.seen = {}
        self.in_order_safe = in_order_safe


class Sched:
    def __init__(self):
        self.E = {"pe": Eng("pe", True), "act": Eng("act"), "dve": Eng("dve"), "pool": Eng("pool"), "sp": Eng("sp")}
        self.sems = {}
        self.dma_cnt = {}
        self.n_ops = 0
        self.n_waits = 0

    def _need(self, eng, evs):
        need = {}
        for (k, v) in evs:
            if eng.seen.get(k, 0) >= v:
                continue
            if need.get(k, 0) < v:
                need[k] = v
        return need

    def _emit_waits(self, eng, need):
        for k, v in need.items():
            h = self.sems[k]
            eng.ops.append(lambda e, h=h, v=v: e.wait_ge(h, v))
            eng.seen[k] = v
            self.n_waits += 1

    def _record(self, ev, reads, writes):
        for b in reads:
            b.r.append(ev)
            if len(b.r) > 64:
                mx = {}
                for (k, v) in b.r:
                    if mx.get(k, 0) < v:
                        mx[k] = v
                b.r = list(mx.items())
        for b in writes:
            b.w = ev
            b.r = []

    def op(self, ename, fn, reads=(), writes=(), inc=True):
        if ename == "pool" and "nopool" in KDBG:
            ename = "dve"
        eng = self.E[ename]
        if ename != "pe":
            ex = [b for b in reads if b.excl]
            if ex:
                reads = [b for b in reads if not b.excl]
                writes = list(writes) + ex
        evs = []
        for b in reads:
            if b.w is not None:
                evs.append(b.w)
        for b in writes:
            if b.w is not None and (STRICT or b.w[0] != eng.sem):
                evs.append(b.w)
            for ev in b.r:
                if STRICT or ev[0] != eng.sem:
                    evs.append(ev)
        if eng.in_order_safe:
            evs = [e for e in evs if e[0] != eng.sem]
        self._emit_waits(eng, self._need(eng, evs))
        self.n_ops += 1
        if inc:
            eng.count += 1
            ev = (eng.sem, eng.count)
            h = self.sems[eng.sem]
            eng.ops.append(lambda e, fn=fn, h=h: fn(e).then_inc(h, 1))
        else:
            ev = (eng.sem, eng.count + 1)
            eng.ops.append(lambda e, fn=fn: fn(e))
        self._record(ev, reads, writes)
        return ev

    def dma(self, qname, semkey, fn, reads=(), writes=()):
        eng = self.E[qname]
        evs = []
        for b in reads:
            if b.w is not None:
                evs.append(b.w)
        for b in writes:
            if b.w is not None:
                evs.append(b.w)
            evs.extend(b.r)
        self._emit_waits(eng, self._need(eng, evs))
        self.dma_cnt[semkey] = self.dma_cnt.get(semkey, 0) + 16
        ev = (semkey, self.dma_cnt[semkey])
        h = self.sems[semkey]
        eng.ops.append(lambda e, fn=fn, h=h: fn(e).then_inc(h, 16))
        self.n_ops += 1
        self._record(ev, reads, writes)
        return ev

    def wait_event(self, ename, ev):
        eng = self.E[ename]
        self._emit_waits(eng, self._need(eng, [ev]))

    def replay(self, block):
        E = self.E

        @block.tensor
        def _(e):
            for f in E["pe"].ops:
                f(e)

        @block.scalar
        def _(e):
            for f in E["act"].ops:
                f(e)

        @block.vector
        def _(e):
            for f in E["dve"].ops:
                f(e)

        @block.gpsimd
        def _(e):
            for f in E["pool"].ops:
                f(e)

        @block.sync
        def _(e):
            for f in E["sp"].ops:
                f(e)


def build_nc(SEQ, STOP=99):
    NTILE = SEQ // TT
    nc = bass.Bass("TRN2", target_bir_lowering=False)

    def din(name, shape, dt=F32):
        return nc.dram_tensor(name, list(shape), dt, kind="ExternalInput").ap()

    def dout(name, shape):
        return nc.dram_tensor(name, list(shape), F32, kind="ExternalOutput").ap()

    xp = din("xp", [SEQ, 1024]); xs = din("xs", [32, 1024])
    ck = din("ck", [2, 1024, 1024]); cv = din("cv", [2, 1024, 1024]); sth = din("sth", [2, 8, 128, 128])
    norm_g = din("norm_g", [8, 1024]); w_in = din("w_in", [1024, 4096]); lbl = din("lbl", [2, 1024])
    onorm = din("onorm", [1024]); w_ho = din("w_ho", [1024, 1024]); kvg = din("kvg", [1024])
    w_kv = din("w_kv", [1024, 2048]); w_dq = din("w_dq", [1024, 1024]); dlam = din("dlam", [256])
    subg = din("subg", [128]); w_do = din("w_do", [1024, 1024])
    w_up = din("w_up", [2, 1024, 4096]); w_down = din("w_down", [2, 4096, 1024])
    c_ident = din("c_ident", [128, 128]); c_scanm = din("c_scanm", [128, 512]); c_mask2 = din("c_mask2", [128, 512])

    yp = dout("yp", [SEQ, 1024]); ys = dout("ys", [32, 1024]); kp = dout("kp", [SEQ, 1024]); vp = dout("vp", [SEQ, 1024])
    stp = dout("stp", [8, 128, 128]); ks = dout("ks", [32, 1024]); vs = dout("vs", [32, 1024]); sts = dout("sts", [2, 8, 128, 128])

    wbf = nc.dram_tensor("wbf", [NPIECE, 128, 4096], BF16, kind="Internal").ap()
    KTs = nc.dram_tensor("KTs", [8, 128, SEQ], BF16, kind="Internal").ap()
    Vs = nc.dram_tensor("Vs", [8, 128, SEQ // 128, 128], BF16, kind="Internal").ap()

    pieces = []
    for cg in range(8):
        pieces.append((w_in, 0, cg))
    for cg in range(2):
        pieces.append((w_ho, 0, cg))
    for cg in range(8):
        pieces.append((w_up[0], 0, cg))
    for cg in range(2):
        for kg in range(4):
            pieces.append((w_down[0], kg, cg))
    for cg in range(4):
        pieces.append((w_kv, 0, cg))
    for cg in range(2):
        pieces.append((w_dq, 0, cg))
    for cg in range(2):
        pieces.append((w_do, 0, cg))
    for cg in range(8):
        pieces.append((w_up[1], 0, cg))
    for cg in range(2):
        for kg in range(4):
            pieces.append((w_down[1], kg, cg))
    assert len(pieces) == NPIECE
    tile_seq = ([0, 1, 2, 3, 6, 7, 4, 5, 8, 9] + list(range(10, 26)) + [26, 27, 28, 29, 26, 27, 30, 31, 32, 33]
                + list(range(34, 50)))
    n_tiles_total = NTILE + 2
    use_seq = tile_seq * n_tiles_total

    with ExitStack() as st:
        def sb(name, shape, dt):
            return st.enter_context(nc.sbuf_tensor(name, list(shape), dt))

        ident_f = sb("ident_f", [128, 128], F32); ident_b = sb("ident_b", [128, 128], BF16)
        ones_f = sb("ones_f", [128, 128], F32); ones_b = sb("ones_b", [128, 128], BF16); onesD = sb("onesD", [128, 128], BF16); onesH = sb("onesH", [128, 128], BF16)
        scanm = sb("scanm", [128, 512], F32); mask2 = sb("mask2", [128, 512], F32)
        G = sb("G", [128, 80], F32); LB = sb("LB", [128, 64], F32); SG = sb("SG", [128, 4], F32)
        DL = sb("DL", [128, 512], F32); epst = sb("epst", [128, 1], F32)
        hT = sb("hT", [128, 8, 512], F32); aT = sb("aT", [128, 8, 512], BF16); aT2 = sb("aT2", [128, 8, 512], BF16)
        sqs = sb("sqs", [128, 2, 512], BF16); rstd = sb("rstd", [128, 2, 512], F32)
        mT = sb("mT", [128, 8, 512], F32)
        ost = sb("ost", [128, 4, 1024], F32)
        Vtok = sb("Vtok", [128, 4, 1024], BF16)
        R = sb("R", [128, 16, 1024], BF16)
        TH = sb("TH", [128, 2, 4, 512], F32)
        QK = sb("QK", [128, 2, 3, 512], BF16)
        ATm = sb("ATm", [128, 2, 512], BF16); Ktok = sb("Ktok", [128, 2, 512], BF16)
        Sst = sb("Sst", [128, 8, 128], F32); Sall = sb("Sall", [128, 9, 128], F32); Sbf = sb("Sbf", [128, 2, 8, 128], BF16)
        KT = sb("KT", [128, 8, 512], BF16); QT = sb("QT", [128, 8, 512], BF16)
        Wt = sb("Wt", [128, NW, 4096], BF16)
        PSQ = [st.enter_context(nc.psum_tensor(f"psq{i}", [128, 1024], F32)) for i in range(4)]

        S = Sched()
        semnames = (["pe", "act", "dve", "pool", "sp", "cst", "xld", "kts", "vss0", "vss1", "vss2", "vss3", "stl", "sto", "cvl0", "cvl1"]
                    + [f"w{i}" for i in range(NW)] + [f"ws{i}" for i in range(NW)]
                    + [f"rk{i}" for i in range(4)] + [f"rv{i}" for i in range(4)] + [f"os{i}" for i in range(4)])
        S.sems = {n: st.enter_context(nc.semaphore(n)) for n in semnames}
        block = st.enter_context(nc.Block())

        hB = [Buf(f"h{i}") for i in range(8)]; aB = [Buf(f"a{i}") for i in range(8)]; a2B = [Buf(f"a2{i}") for i in range(8)]
        mB = [Buf(f"m{i}") for i in range(8)]; sqB = [Buf("sq0"), Buf("sq1")]; rsB = [Buf("rs0"), Buf("rs1")]
        ostB = [Buf(f"ost{i}") for i in range(4)]; vtB = [Buf(f"vt{i}") for i in range(4)]
        RB = [Buf(f"R{i}") for i in range(16)]
        THB = [[Buf(f"TH{p}{i}") for i in range(4)] for p in range(2)]
        QKB = [[Buf(f"QK{p}{i}") for i in range(3)] for p in range(2)]
        ATB = [Buf("AT0"), Buf("AT1")]; KtB = [Buf("Kt0"), Buf("Kt1")]
        SB = [Buf(f"S{i}") for i in range(8)]; SallB = Buf("Sall"); SbfB = [Buf("Sbf0"), Buf("Sbf1")]
        KTB = [Buf(f"KT{i}") for i in range(8)]; QB = [Buf(f"Q{i}") for i in range(8)]
        WB = [Buf(f"W{i}") for i in range(NW)]
        BP = [Buf(f"bank{i}", excl=True) for i in range(8)]
        cB = Buf("consts")
        wbfB = [Buf(f"wbf{i}") for i in range(NPIECE)]
        ktsB = [Buf(f"kts{i}") for i in range(max(NTILE, 1))]; vsB = [Buf(f"vs{i}") for i in range(max(NTILE, 1))]
        stB = Buf("stout")

        def bank(i):
            return PSQ[i // 2][:, (i % 2) * 512:(i % 2) * 512 + 512]

        bank_rr = [0]

        def nextbank():
            b = bank_rr[0]
            bank_rr[0] = (b + 1) % 7
            return b

        def MM(out, lhsT, rhs, start, stop, reads, writes, inc=True, skip=False):
            if skip:
                S.op("pe", lambda e: e.matmul(out, lhsT, rhs, start=start, stop=stop, skip_group_check=True), reads, writes, inc)
            else:
                S.op("pe", lambda e: e.matmul(out, lhsT, rhs, start=start, stop=stop), reads, writes, inc)

        def TR(out, in_, ident, reads, writes, inc=True):
            S.op("pe", lambda e: e.transpose(out, in_, ident), reads, writes, inc)

        def ACT(out, in_, func, reads, writes, scale=1.0, bias=None):
            if bias is None:
                S.op("act", lambda e: e.activation(out=out, in_=in_, func=func, scale=scale), reads, writes)
            else:
                S.op("act", lambda e: e.activation(out=out, in_=in_, func=func, scale=scale, bias=bias), reads, writes)

        def CP(eng, out, in_, reads, writes):
            if eng == "act":
                S.op("act", lambda e: e.copy(out, in_), reads, writes)
            else:
                S.op(eng, lambda e: e.tensor_copy(out, in_), reads, writes)

        def TTo(eng, out, in0, in1, op, reads, writes):
            S.op(eng, lambda e: e.tensor_tensor(out=out, in0=in0, in1=in1, op=op), reads, writes)

        def TS(eng, out, in0, s1, s2, op0, op1, reads, writes):
            S.op(eng, lambda e: e.tensor_scalar(out=out, in0=in0, scalar1=s1, scalar2=s2, op0=op0, op1=op1), reads, writes)

        def STT(eng, out, in0, scalar, in1, op0, op1, reads, writes):
            eng = "dve"
            S.op(eng, lambda e: e.scalar_tensor_tensor(out=out, in0=in0, scalar=scalar, in1=in1, op0=op0, op1=op1), reads, writes)

        def MEMSET(eng, ap, val, writes):
            S.op(eng, lambda e: e.memset(ap, val), (), writes)

        def DMA(q, sem, out, in_, reads, writes, slow=False):
            if slow:
                return S.dma(q, sem, lambda e: e.dma_start(out=out, in_=in_, allow_slow_non_contiguous=True), reads, writes)
            return S.dma(q, sem, lambda e: e.dma_start(out=out, in_=in_), reads, writes)

        MEMSET("pool", ones_f[:], 1.0, [cB]); MEMSET("pool", ones_b[:], 1.0, [cB]); MEMSET("pool", onesD[:], 1.0 / 1024.0, [cB]); MEMSET("pool", onesH[:], 1.0 / 128.0, [cB])
        MEMSET("pool", epst[:], EPS, [cB])
        ncst = 0
        DMA("sp", "cst", ident_f[:], c_ident[:, :], [], [cB]); ncst += 1
        DMA("sp", "cst", scanm[:], c_scanm[:, :], [], [cB]); ncst += 1
        DMA("sp", "cst", mask2[:], c_mask2[:, :], [], [cB]); ncst += 1
        DMA("sp", "cst", G[:, 0:64].rearrange("p (r c) -> p r c", c=8), norm_g.rearrange("r (c p) -> p r c", p=128), [], [cB], slow=True); ncst += 1
        DMA("sp", "cst", G[:, 64:72], onorm.rearrange("(c p) -> p c", p=128), [], [cB], slow=True); ncst += 1
        DMA("sp", "cst", G[:, 72:80], kvg.rearrange("(c p) -> p c", p=128), [], [cB], slow=True); ncst += 1
        DMA("sp", "cst", LB[:, 0:16].rearrange("p (l c) -> p l c", c=8), lbl.rearrange("l (c p) -> p l c", p=128), [], [cB], slow=True); ncst += 1
        DMA("sp", "cst", SG[:, 0:1], subg.rearrange("(p o) -> p o", o=1), [], [cB], slow=True); ncst += 1
        DMA("sp", "cst", DL[:, 0:256], bass.AP(dlam.tensor, 0, [[0, 128], [1, 256]]), [], [cB], slow=True); ncst += 1
        cB.w = ("cst", 16 * ncst)
        cB.r = []
        CP("dve", ident_b[:], ident_f[:], [cB], [cB])
        ACT(LB[:, 0:16], LB[:, 0:16], AF.Exp, [cB], [cB])
        TTo("dve", LB[:, 16:24], LB[:, 0:8], LB[:, 8:16], ALU.add, [cB], [cB])
        S.op("dve", lambda e: e.reciprocal(LB[:, 16:24], LB[:, 16:24]), [cB], [cB])
        TTo("dve", LB[:, 24:32], LB[:, 0:8], LB[:, 16:24], ALU.mult, [cB], [cB])
        TTo("dve", LB[:, 32:40], LB[:, 8:16], LB[:, 16:24], ALU.mult, [cB], [cB])
        S.op("dve", lambda e: e.tensor_scalar_mul(LB[:, 40:48], LB[:, 32:40], -1.0), [cB], [cB])
        TTo("dve", DL[:, 256:320], DL[:, 0:64], DL[:, 64:128], ALU.mult, [cB], [cB])
        TTo("dve", DL[:, 320:384], DL[:, 128:192], DL[:, 192:256], ALU.mult, [cB], [cB])
        S.op("dve", lambda e: e.reduce_sum(DL[:, 384:386], DL[:, 256:384].rearrange("p (a b) -> p a b", b=64), AX.X), [cB], [cB])
        ACT(DL[:, 384:386], DL[:, 384:386], AF.Exp, [cB], [cB])
        TTo("dve", SG[:, 1:2], DL[:, 385:386], DL[:, 384:385], ALU.subtract, [cB], [cB])
        S.op("dve", lambda e: e.tensor_scalar_add(SG[:, 1:2], SG[:, 1:2], -LAM_INIT), [cB], [cB])
        S.op("dve", lambda e: e.tensor_scalar_mul(SG[:, 2:3], SG[:, 0:1], 1.0 - LAM_INIT), [cB], [cB])
        lb_ap = lambda h: LB[:, 24 + h:25 + h]
        oml_ap = lambda h: LB[:, 32 + h:33 + h]
        noml_ap = lambda h: LB[:, 40 + h:41 + h]
        neg_lam = SG[:, 1:2]; subgs = SG[:, 2:3]

        def gcol(r, c):
            return G[:, r * 8 + c:r * 8 + c + 1]

        stg = [hT[:, :, :], mT[:, :, :]]
        stgB = [hB, mB]
        cast_eng = ["dve", "act", "pool"]
        for i, (wap, kg, cg) in enumerate(pieces):
            sI = i % 2
            src = wap[kg * 1024:(kg + 1) * 1024, cg * 512:(cg + 1) * 512].rearrange("(kc p) j -> p kc j", p=128)
            DMA("sp", f"cvl{sI}", stg[sI], src, [], stgB[sI])
            slot = i % NW
            wv = Wt[:, slot, :].rearrange("p (kc j) -> p kc j", j=512)
            CP(cast_eng[i % 3], wv, stg[sI], stgB[sI], [WB[slot]])
            DMA("pool", f"ws{slot}", wbf[i], Wt[:, slot, :], [WB[slot]], [wbfB[i]])

        ring = {"issued": 0, "cur": 0}

        def ring_issue(i):
            slot = i % NW
            DMA("sp", f"w{slot}", Wt[:, slot, :], wbf[use_seq[i]], [wbfB[use_seq[i]]], [WB[slot]])

        def next_piece(expect):
            while ring["issued"] < min(len(use_seq), ring["cur"] + NW):
                ring_issue(ring["issued"])
                ring["issued"] += 1
            i = ring["cur"]
            assert use_seq[i] == expect, (i, use_seq[i], expect)
            ring["cur"] += 1
            slot = i % NW
            return Wt[:, slot, :].rearrange("p (kc j) -> p kc j", j=512), WB[slot]

        def proj_fm(inT, inB, N, piece_ids, evac):
            for pi, pid in enumerate(piece_ids):
                wap, wB = next_piece(pid)
                for o in range(4):
                    b = nextbank()
                    for kc in range(8):
                        MM(bank(b)[:, :N], wap[:, kc, o * 128:(o + 1) * 128], inT[:, kc, :N], kc == 0, kc == 7,
                           [wB, inB[kc]], [BP[b]], inc=(kc == 7))
                    evac(pi * 4 + o, bank(b)[:, :N], BP[b])

        def proj_down(uT, uB, N, piece_base, evac):
            for cg in range(2):
                bs = [nextbank() for _ in range(4)]
                for kg in range(4):
                    wap, wB = next_piece(piece_base + cg * 4 + kg)
                    for o in range(4):
                        for kc in range(8):
                            j = kg * 8 + kc
                            MM(bank(bs[o])[:, :N], wap[:, kc, o * 128:(o + 1) * 128], uT(j)[:, :N],
                               (kg == 0 and kc == 0), (kg == 3 and kc == 7), [wB, uB(j)], [BP[bs[o]]],
                               inc=(kc == 7))
                for o in range(4):
                    evac(cg * 4 + o, bank(bs[o])[:, :N], BP[bs[o]])

        def proj_tm(inT, inB, N, pid, evac):
            wap, wB = next_piece(pid)
            nsub = (N + 127) // 128
            subr = min(N, 128)
            for s in range(nsub):
                b = nextbank()
                for kc in range(8):
                    MM(bank(b)[:subr, :], inT[:, kc, s * 128:s * 128 + subr], wap[:, kc, :], kc == 0, kc == 7,
                       [wB, inB[kc]], [BP[b]], inc=(kc == 7))
                evac(s, bank(b)[:subr, :], BP[b])

        def norm_rstd(chunks, ones_ap, N, ri, psum_src=True):
            b = nextbank()
            n = len(chunks)
            for c, (ap, Bs) in enumerate(chunks):
                sl = c % 2
                if psum_src or c % 2 == 0:
                    ACT(sqs[:, sl, :N], ap, AF.Square, Bs, [sqB[sl]])
                else:
                    TTo("pool", sqs[:, sl, :N], ap, ap, ALU.mult, Bs, [sqB[sl]])
                MM(bank(b)[:, :N], ones_ap[:, :], sqs[:, sl, :N], c == 0, c == n - 1, [sqB[sl], cB], [BP[b]], inc=True)
            ACT(rstd[:, ri, :N], bank(b)[:, :N], AF.Ln, [BP[b], cB], [rsB[ri]], bias=epst[:, 0:1])
            ACT(rstd[:, ri, :N], rstd[:, ri, :N], AF.Exp, [rsB[ri]], [rsB[ri]], scale=-0.5)

        def h_chunks(N):
            return [(hT[:, c, :N], [hB[c]]) for c in range(8)]

        def m_chunks(N):
            return [(mT[:, c, :N], [mB[c]]) for c in range(8)]

        def apply_norm(dstT, dstB, grow, N, ri):
            for c in range(8):
                eng = "dve" if c % 2 == 0 else "pool"
                STT(eng, dstT[:, c, :N], hT[:, c, :N], gcol(grow, c), rstd[:, ri, :N], ALU.mult, ALU.mult,
                    [hB[c], rsB[ri], cB], [dstB[c]])

        def resid_add(grow, N):
            ACT(rstd[:, 0, :N], bank(7)[:, :N], AF.Ln, [BP[7], cB], [rsB[0]], bias=epst[:, 0:1])
            ACT(rstd[:, 0, :N], rstd[:, 0, :N], AF.Exp, [rsB[0]], [rsB[0]], scale=-0.5)
            for c in range(8):
                eng = "dve" if c % 2 == 0 else "pool"
                TTo(eng, mT[:, c, :N], mT[:, c, :N], rstd[:, 0, :N], ALU.mult, [mB[c], rsB[0]], [mB[c]])
                STT(eng, hT[:, c, :N], mT[:, c, :N], gcol(grow, c), hT[:, c, :N], ALU.mult, ALU.add,
                    [mB[c], hB[c], cB], [hB[c]])

        def stat_mm(idx, N):
            sl = idx % 2
            MM(bank(7)[:, :N], onesD[:, :], sqs[:, sl, :N], idx == 0, idx == 7, [sqB[sl], cB], [BP[7]], inc=True)

        def evac_to_m(idx, ps, pB, N):
            ACT(sqs[:, idx % 2, :N], ps, AF.Square, [pB], [sqB[idx % 2]])
            if idx >= 1:
                stat_mm(idx - 1, N)
            if idx == 7:
                stat_mm(7, N)
            if idx % 2 == 0:
                CP("act", mT[:, idx, :N], ps, [pB], [mB[idx]])
            else:
                CP("dve", mT[:, idx, :N], ps, [pB], [mB[idx]])

        def Rf32(slot):
            return R[:, slot, :].bitcast(F32)

        def mlp(layer, N, up_base, down_base):
            norm_rstd(h_chunks(N), onesD, N, 0)
            apply_norm(aT, aB, layer * 4 + 2, N, 0)

            def ev_up(j, ps, pB):
                slot = j // 2
                half = (j % 2) * 512
                tmp = TH[:, j % 2, 0, :].bitcast(BF16)[:, 0:512]
                ACT(tmp[:, :N], ps, AF.Relu, [pB], [THB[j % 2][0]])
                TTo("pool" if j % 2 == 0 else "dve", R[:, slot, half:half + N], tmp[:, :N], tmp[:, :N], ALU.mult,
                    [THB[j % 2][0]], [RB[slot]])
            proj_fm(aT, aB, N, list(range(up_base, up_base + 8)), ev_up)
            proj_down(lambda j: R[:, j // 2, (j % 2) * 512:(j % 2) * 512 + 512], lambda j: RB[j // 2], N, down_base,
                      lambda idx, ps, pB: evac_to_m(idx, ps, pB, N))
            resid_add(layer * 4 + 3, N)

        def load_x(src_rows, N):
            nsub = (N + 127) // 128
            subr = min(N, 128)
            xv = mT[:, :, :].rearrange("p a b -> p (a b)").rearrange("p (s f) -> p s f", f=1024)
            if N >= 128:
                DMA("sp", "xld", xv[:, :nsub, :], src_rows.rearrange("(s p) f -> p s f", p=128), [], mB)
            else:
                DMA("sp", "xld", xv[:subr, 0, :], src_rows, [], mB)
            for c in range(8):
                b = nextbank()
                for s in range(nsub):
                    TR(bank(b)[:, s * 128:s * 128 + subr], xv[:subr, s, c * 128:(c + 1) * 128], ident_f[:subr, :subr],
                       mB + [cB], [BP[b]], inc=(s == nsub - 1))
                CP("act" if c % 2 == 0 else "dve", hT[:, c, :N], bank(b)[:, :N], [BP[b]], [hB[c]])

        def store_rows_from_fm(srcT, srcB, N, dst_rows):
            nsub = (N + 127) // 128
            subr = min(N, 128)
            for s in range(nsub):
                slot = ost_rr[0]
                ost_rr[0] = (slot + 1) % 4
                for half in range(2):
                    b = nextbank()
                    for cc in range(4):
                        c = half * 4 + cc
                        TR(bank(b)[:subr, cc * 128:(cc + 1) * 128], srcT[:, c, s * 128:s * 128 + subr], ident_f[:, :],
                           [srcB[c], cB], [BP[b]], inc=(cc == 3))
                    CP("act" if half == 0 else "dve", ost[:subr, slot, half * 512:(half + 1) * 512], bank(b)[:subr, :],
                       [BP[b]], [ostB[slot]])
                DMA("pool", f"os{slot}", dst_rows[s * 128:s * 128 + subr, :], ost[:subr, slot, :], [ostB[slot]], [])

        ost_rr = [0]

        def hgrn(N, CH):
            npair = (N + 127) // 128
            pw = min(N, 128)
            nch = N // CH
            norm_rstd(h_chunks(N), onesD, N, 0)
            apply_norm(aT, aB, 0, N, 0)
            def ev_q(j, ps, pB):
                ACT(R[:, 8 + j // 2, (j % 2) * 512:(j % 2) * 512 + N], ps, AF.Silu, [pB], [RB[8 + j // 2]])
            proj_fm(aT, aB, N, [0, 1], ev_q)
            def ev_f(j, ps, pB):
                ACT(Rf32(j)[:, :N], ps, AF.Sigmoid, [pB], [RB[j]])
            proj_fm(aT, aB, N, [2, 3], ev_f)
            def ev_g(j, ps, pB):
                ACT(R[:, 12 + j // 2, (j % 2) * 512:(j % 2) * 512 + N], ps, AF.Silu, [pB], [RB[12 + j // 2]])
            proj_fm(aT, aB, N, [6, 7], ev_g)
            subr = min(N, 128)
            for half in range(2):
                def ev_v(s, ps, pB, half=half):
                    CP("dve" if s % 2 == 0 else "act", Vtok[:subr, s, half * 512:(half + 1) * 512], ps, [pB], [vtB[s]])
                proj_tm(aT, aB, N, 4 + half, ev_v)

            def stageA(h):
                pb = h % 2
                T = lambda i: TH[:, pb, i, :N]
                TB = THB[pb]
                sig = Rf32(h)[:, :N]
                sq_ = R[:, 8 + h // 2, (h % 2) * 512:(h % 2) * 512 + N]
                TS("dve", T(0), sig, oml_ap(h), lb_ap(h), ALU.mult, ALU.add, [RB[h], cB], [TB[0]])
                TS("pool", T(1), sig, noml_ap(h), oml_ap(h), ALU.mult, ALU.add, [RB[h], cB], [TB[1]])
                ACT(T(0), T(0), AF.Ln, [TB[0]], [TB[0]])
                S.op("dve", lambda e: e.tensor_tensor_scan(out=T(2), data0=scanm[:, :N], data1=T(0), initial=0.0,
                                                           op0=ALU.mult, op1=ALU.add), [TB[0], cB], [TB[2]])
                ACT(T(0), T(2), AF.Exp, [TB[2]], [TB[0]])
                ACT(T(3), T(2), AF.Exp, [TB[2]], [TB[3]], scale=-1.0)
                Qt = QK[:, pb, 0, :N]; Kt = QK[:, pb, 1, :N]; Kh = QK[:, pb, 2, :N]
                TTo("dve", Qt, sq_, T(0), ALU.mult, [RB[8 + h // 2], TB[0]], [QKB[pb][0]])
                TTo("pool", Kt, T(1), T(3), ALU.mult, [TB[1], TB[3]], [QKB[pb][1]])
                Ev = TH[:, pb, 0, :N].rearrange("p (c t) -> p c t", t=CH)
                TTo("pool", Kh.rearrange("p (c t) -> p c t", t=CH), Kt.rearrange("p (c t) -> p c t", t=CH),
                    Ev[:, :, CH - 1:CH].broadcast_to([128, nch, CH]), ALU.mult, [QKB[pb][1], TB[0]], [QKB[pb][2]])

            def stageA2(h):
                pb = h % 2
                Qt = QK[:, pb, 0, :N]; Kt = QK[:, pb, 1, :N]; Kh = QK[:, pb, 2, :N]
                for j in range(npair):
                    MM(bank(0)[:pw, j * 128:j * 128 + pw], Kt[:, j * 128:j * 128 + pw], Qt[:, j * 128:j * 128 + pw], True, True,
                       [QKB[pb][0], QKB[pb][1]], [BP[0]], inc=(j == npair - 1))
                TTo("dve", ATm[:pw, pb, :N], bank(0)[:pw, :N], mask2[:pw, :N], ALU.mult, [BP[0], cB], [ATB[pb]])
                pT = bank(1).bitcast(BF16)
                for j in range(npair):
                    TR(pT[:pw, j * 128:(j + 1) * 128], Kh[:, j * 128:j * 128 + pw], ident_b[:, :], [QKB[pb][2], cB], [BP[1]],
                       inc=(j == npair - 1))
                CP("act", Ktok[:pw, pb, :npair * 128], pT[:pw, :npair * 128], [BP[1]], [KtB[pb]])

            def stageA3(h):
                pb = h % 2
                TB = THB[pb]
                dsl = lambda c: PSQ[1][:, (c % 2) * 512 + (c // 2) * 128:(c % 2) * 512 + (c // 2) * 128 + 128]
                for c in range(nch):
                    j = (c * CH) // 128
                    r0 = (c * CH) % 128
                    bb = 2 + c % 2
                    MM(dsl(c), Ktok[r0:r0 + CH, pb, j * 128:(j + 1) * 128],
                       Vtok[r0:r0 + CH, j, h * 128:(h + 1) * 128], True, True, [KtB[pb], vtB[j]], [BP[bb]],
                       inc=(c >= nch - 2))
                CP("pool", Sall[:, 0, :], Sst[:, h, :], [SB[h]], [SallB])
                for c in range(nch):
                    bb = 2 + c % 2
                    STT("dve", Sall[:, c + 1, :], Sall[:, c, :], TH[:, pb, 0, c * CH + CH - 1:c * CH + CH],
                        dsl(c), ALU.mult, ALU.add, [SallB, TB[0], BP[bb]], [SallB])
                CP("pool", Sst[:, h, :], Sall[:, nch, :], [SallB], [SB[h]])
                CP("act", Sbf[:, pb, :nch, :], Sall[:, 0:nch, :], [SallB], [SbfB[pb]])

            def stageB(h):
                pb = h % 2
                Qt = QK[:, pb, 0, :N]
                for j in range(npair):
                    MM(bank(4)[:, j * 128:j * 128 + pw], Vtok[:pw, j, h * 128:(h + 1) * 128], ATm[:pw, pb, j * 128:j * 128 + pw],
                       j == 0, False, [vtB[j], ATB[pb]], [BP[4]], inc=False, skip=True)
                for c in range(nch):
                    MM(bank(4)[:, c * CH:(c + 1) * CH], Sbf[:, pb, c, :], Qt[:, c * CH:(c + 1) * CH], False, c == nch - 1,
                       [SbfB[pb], QKB[pb][0]], [BP[4]], inc=(c == nch - 1), skip=True)
                ACT(sqs[:, 0, :N], bank(4)[:, :N], AF.Square, [BP[4]], [sqB[0]])
                MM(bank(5)[:, :N], onesH[:, :], sqs[:, 0, :N], True, True, [sqB[0], cB], [BP[5]])
                ACT(rstd[:, 1, :N], bank(5)[:, :N], AF.Ln, [BP[5], cB], [rsB[1]], bias=epst[:, 0:1])
                ACT(rstd[:, 1, :N], rstd[:, 1, :N], AF.Exp, [rsB[1]], [rsB[1]], scale=-0.5)

            def stageB2(h):
                pb = h % 2
                tmp = TH[:, pb, 2, :N]
                STT("dve", tmp, bank(4)[:, :N], G[:, 64 + h:65 + h], rstd[:, 1, :N], ALU.mult, ALU.mult,
                    [BP[4], rsB[1], cB], [THB[pb][2]])
                sg_ = R[:, 12 + h // 2, (h % 2) * 512:(h % 2) * 512 + N]
                TTo("pool", aT2[:, h, :N], tmp, sg_, ALU.mult, [THB[pb][2], RB[12 + h // 2]], [a2B[h]])

            stageA(0); stageA2(0)
            for h in range(8):
                if h + 1 < 8:
                    stageA(h + 1)
                if h >= 1:
                    stageB2(h - 1)
                stageA3(h)
                if h + 1 < 8:
                    stageA2(h + 1)
                stageB(h)
            stageB2(7)
            proj_fm(aT2, a2B, N, [8, 9], lambda idx, ps, pB: evac_to_m(idx, ps, pB, N))
            resid_add(1, N)

        def attend(h, N, kblocks):
            nkb = len(kblocks)

            Pbufs = [(TH[:, 0, 0, :].bitcast(BF16), [THB[0][0]]), (TH[:, 1, 0, :].bitcast(BF16), [THB[1][0]]),
                     (QK[:, 0, 0:2, :].rearrange("p a b -> p (a b)"), [QKB[0][0], QKB[0][1]])]

            def qk(i):
                kb = kblocks[i]
                par = i % 2
                q = PSQ[par]
                cl = kb["cl"]; nk = kb["nk"]
                P, PBl = Pbufs[i % 3]
                MM(q[:nk, cl:N], kb["KT"][0:64, :nk], QT[0:64, h, cl:N], True, True, kb["rk"] + [QB[h]], [BP[2 * par]], inc=False)
                MM(q[:nk, 512 + cl:512 + N], kb["KT"][64:128, :nk], QT[64:128, h, cl:N], True, True, kb["rk"] + [QB[h]],
                   [BP[2 * par + 1]], inc=True)
                qv = q[:nk, :].rearrange("p (two n) -> p two n", two=2)[:, :, cl:N]
                pv = P[:nk, :].rearrange("p (two n) -> p two n", two=2)[:, :, cl:N]
                ACT(pv, qv, AF.Exp, [BP[2 * par], BP[2 * par + 1]], PBl, scale=0.125)
                if kb["corner"]:
                    cv_ = P[64:128, :].rearrange("p (two n) -> p two n", two=2)[:, :, cl:cl + 64]
                    MEMSET("pool", cv_, 0.0, PBl)

            acc2 = TH[:, 1, 3, :]

            def pv(i):
                kb = kblocks[i]
                cl = kb["cl"]; nk = kb["nk"]
                P, PBl = Pbufs[i % 3]
                first = (i == 0); last = (i == nkb - 1)
                MM(bank(4)[:, cl:N], kb["V"][:nk, :], P[:nk, cl:N], first, last, kb["rv"] + PBl, [BP[4]], inc=False)
                MM(bank(5)[:, cl:N], kb["V"][:nk, :], P[:nk, 512 + cl:512 + N], first, last, kb["rv"] + PBl, [BP[5]], inc=False)
                MM(bank(6)[:, cl:N], ones_b[:nk, :], P[:nk, cl:N], first, last, PBl + [cB], [BP[6]], inc=True)
                if first:
                    CP("dve", acc2[:nk, cl:N], P[:nk, 512 + cl:512 + N], PBl, [THB[1][3]])
                else:
                    TTo("dve", acc2[:nk, cl:N], acc2[:nk, cl:N], P[:nk, 512 + cl:512 + N], ALU.add, PBl + [THB[1][3]], [THB[1][3]])
                if last:
                    MM(bank(7)[:, :N], ones_f[:, :], acc2[:, :N], True, True, [THB[1][3], cB], [BP[7]], inc=True)

            qk(0)
            if nkb > 1:
                qk(1)
            for i in range(nkb):
                if i + 2 < nkb:
                    qk(i + 2)
                pv(i)
            r1 = TH[:, 0, 1, :N]; r2 = TH[:, 0, 2, :N]; o1 = TH[:, 1, 1, :N]; o2 = TH[:, 1, 2, :N]
            ACT(r1, bank(6)[:, :N], AF.Ln, [BP[6]], [THB[0][1]])
            ACT(r2, bank(7)[:, :N], AF.Ln, [BP[7]], [THB[0][2]])
            CP("dve", o1, bank(4)[:, :N], [BP[4]], [THB[1][1]])
            CP("dve", o2, bank(5)[:, :N], [BP[5]], [THB[1][2]])
            ACT(r1, r1, AF.Exp, [THB[0][1]], [THB[0][1]], scale=-1.0)
            ACT(r2, r2, AF.Exp, [THB[0][2]], [THB[0][2]], scale=-1.0)
            TTo("dve", o1, o1, r1, ALU.mult, [THB[1][1], THB[0][1]], [THB[1][1]])
            TTo("dve", o2, o2, r2, ALU.mult, [THB[1][2], THB[0][2]], [THB[1][2]])
            STT("dve", mT[:, h, :N], o2, neg_lam, o1, ALU.mult, ALU.add, [THB[1][1], THB[1][2], cB], [mB[h]])

        def attn_subln(N):
            for h in range(8):
                ACT(sqs[:, h % 2, :N], mT[:, h, :N], AF.Square, [mB[h]], [sqB[h % 2]])
                MM(bank(h)[:, :N], onesH[:, :], sqs[:, h % 2, :N], True, True, [sqB[h % 2], cB], [BP[h]])
            for h in range(8):
                rr = TH[:, h % 2, 3, :N]
                ACT(rr, bank(h)[:, :N], AF.Ln, [BP[h], cB], [THB[h % 2][3]], bias=epst[:, 0:1])
                ACT(rr, rr, AF.Exp, [THB[h % 2][3]], [THB[h % 2][3]], scale=-0.5)
                STT("dve", aT[:, h, :N], mT[:, h, :N], subgs, rr, ALU.mult, ALU.mult, [mB[h], THB[h % 2][3], cB], [aB[h]])

        def diff_layer(N, mode, t=0, j=0):
            nsub = (N + 127) // 128
            subr = min(N, 128)
            norm_rstd(h_chunks(N), onesD, N, 0)
            apply_norm(aT, aB, 4, N, 0)
            apply_norm(aT2, a2B, 9, N, 0)
            kdst = kp[t * TT:(t + 1) * TT, :] if mode == "prompt" else ks[j * 16:(j + 1) * 16, :]
            vdst = vp[t * TT:(t + 1) * TT, :] if mode == "prompt" else vs[j * 16:(j + 1) * 16, :]
            for which, dst in ((0, kdst), (1, vdst)):
                for half in range(2):
                    def ev_kv(s, ps, pB, which=which, half=half):
                        eng = "act" if (s + half) % 2 == 0 else "dve"
                        CP(eng, ost[:subr, s, half * 512:(half + 1) * 512], ps, [pB], [ostB[s]])
                        if which == 1:
                            CP("dve" if eng == "act" else "act", Vtok[:subr, s, half * 512:(half + 1) * 512],
                               ost[:subr, s, half * 512:(half + 1) * 512], [ostB[s]], [vtB[s]])
                    proj_tm(aT2, a2B, N, 26 + which * 2 + half, ev_kv)
                for s in range(nsub):
                    DMA("pool", f"os{s}", dst[s * 128:s * 128 + subr, :], ost[:subr, s, :], [ostB[s]], [])
            if KCUT2 < 2:
                ring["cur"] += 6; ring["issued"] = max(ring["issued"], ring["cur"])
                return
            def ev_k(c, ps, pB):
                CP("act" if c % 2 == 0 else "dve", KT[:, c, :N], ps, [pB], [KTB[c]])
            proj_fm(aT2, a2B, N, [26, 27], ev_k)
            def ev_qq(c, ps, pB):
                CP("dve" if c % 2 == 0 else "act", QT[:, c, :N], ps, [pB], [QB[c]])
            proj_fm(aT, aB, N, [30, 31], ev_qq)
            if mode == "prompt":
                if t < NTILE - 1:
                    DMA("pool", "kts", KTs[:, :, t * TT:(t + 1) * TT].rearrange("h p n -> p h n"), KT[:, :, :], KTB, [ktsB[t]], slow=True)
                    for s in range(4):
                        DMA("pool", f"vss{s}", Vs[:, :, t * 4 + s, :].rearrange("h p e -> p h e"),
                            Vtok[:, s, :].rearrange("p (h e) -> p h e", h=8), [vtB[s]], [vsB[t]], slow=True)
                npast = t * TT
                nblk = (npast + 2047) // 2048
                for h in range(8):
                    kbs = []
                    for blk in range(nblk):
                        rs = (h * nblk + blk) % 4
                        k0 = blk * 2048
                        k1 = min(npast, k0 + 2048)
                        nkb_ = (k1 - k0) // 128
                        Kr = R[:, 4 * rs:4 * rs + 2, :].rearrange("p a b -> p (a b)")
                        Vr = R[:, 4 * rs + 2:4 * rs + 4, :].rearrange("p a b -> p (a b)").rearrange("p (k e) -> p k e", e=128)
                        tl = list(range(k0 // TT, (k1 + TT - 1) // TT))
                        DMA("sp", f"rk{rs}", Kr[:, :k1 - k0], KTs[h, :, k0:k1], [ktsB[x] for x in tl], [RB[4 * rs], RB[4 * rs + 1]])
                        DMA("sp", f"rv{rs}", Vr[:, :nkb_, :], Vs[h, :, k0 // 128:k1 // 128, :], [vsB[x] for x in tl],
                            [RB[4 * rs + 2], RB[4 * rs + 3]])
                        for kb_ in range(nkb_):
                            kbs.append(dict(KT=Kr[:, kb_ * 128:(kb_ + 1) * 128], V=Vr[:, kb_, :], nk=128, cl=0, corner=False,
                                            rk=[RB[4 * rs], RB[4 * rs + 1]], rv=[RB[4 * rs + 2], RB[4 * rs + 3]]))
                    for jj in range(4):
                        kbs.append(dict(KT=KT[:, h, jj * 128:(jj + 1) * 128], V=Vtok[:, jj, h * 128:(h + 1) * 128], nk=128,
                                        cl=128 * jj, corner=True, rk=[KTB[h]], rv=[vtB[jj]]))
                    attend(h, N, kbs)
                attn_subln(N)
            else:
                xv = mT[:, :, :].rearrange("p a b -> p (a b)").rearrange("p (s f) -> p s f", f=1024)
                for half in range(2):
                    DMA("sp", "xld", xv, ck[j, half * 512:(half + 1) * 512, :].rearrange("(s p) f -> p s f", p=128), [], mB)
                    for c in range(8):
                        b = nextbank()
                        for s in range(4):
                            TR(bank(b)[:, s * 128:(s + 1) * 128], xv[:, s, c * 128:(c + 1) * 128], ident_f[:, :], mB + [cB], [BP[b]],
                               inc=(s == 3))
                        CP("act" if c % 2 == 0 else "dve", R[:, c, half * 512:(half + 1) * 512], bank(b), [BP[b]], [RB[c]])
                for half in range(2):
                    DMA("sp", "xld", xv, cv[j, half * 512:(half + 1) * 512, :].rearrange("(s p) f -> p s f", p=128), [], mB)
                    for s in range(4):
                        CP("dve" if s % 2 == 0 else "pool", R[:, 8 + half * 4 + s, :], xv[:, s, :], mB, [RB[8 + half * 4 + s]])
                for h in range(8):
                    kbs = []
                    for kb_ in range(8):
                        kbs.append(dict(KT=R[:, h, kb_ * 128:(kb_ + 1) * 128], V=R[:, 8 + kb_, h * 128:(h + 1) * 128], nk=128, cl=0,
                                        corner=False, rk=[RB[h]], rv=[RB[8 + kb_]]))
                    kbs.append(dict(KT=KT[:, h, 0:N], V=Vtok[:N, 0, h * 128:(h + 1) * 128], nk=N, cl=0, corner=False,
                                    rk=[KTB[h]], rv=[vtB[0]]))
                    attend(h, N, kbs)
                attn_subln(N)
            proj_fm(aT, aB, N, [32, 33], lambda idx, ps, pB: evac_to_m(idx, ps, pB, N))
            resid_add(5, N)

        for h in range(8):
            MEMSET("pool", Sst[:, h, :], 0.0, [SB[h]])
        for t in range(NTILE):
            if STOP < 2:
                break
            load_x(xp[t * TT:(t + 1) * TT, :], TT)
            if STOP < 3:
                store_rows_from_fm(hT, hB, TT, yp[t * TT:(t + 1) * TT, :])
                continue
            hgrn(TT, 64)
            if STOP < 4:
                ring["cur"] += 40; ring["issued"] = max(ring["issued"], ring["cur"])
                store_rows_from_fm(hT, hB, TT, yp[t * TT:(t + 1) * TT, :])
                continue
            mlp(0, TT, 10, 18)
            if STOP < 5:
                ring["cur"] += 24; ring["issued"] = max(ring["issued"], ring["cur"])
                store_rows_from_fm(hT, hB, TT, yp[t * TT:(t + 1) * TT, :])
                continue
            diff_layer(TT, "prompt", t=t)
            if KCUT2 < 2:
                ring["cur"] += 16; ring["issued"] = max(ring["issued"], ring["cur"])
                store_rows_from_fm(hT, hB, TT, yp[t * TT:(t + 1) * TT, :])
                continue
            mlp(1, TT, 34, 42)
            store_rows_from_fm(hT, hB, TT, yp[t * TT:(t + 1) * TT, :])
        evs = [DMA("pool", "sto", stp.rearrange("h k v -> k h v"), Sst[:, :, :], SB, [stB], slow=True)]
        for j in range(2):
            if STOP < 6:
                break
            DMA("sp", "stl", Sst[:, :, :], sth[j].rearrange("h k v -> k h v"), [stB], SB, slow=True)
            load_x(xs[j * 16:(j + 1) * 16, :], 16)
            hgrn(16, 16)
            mlp(0, 16, 10, 18)
            diff_layer(16, "sample", j=j)
            mlp(1, 16, 34, 42)
            store_rows_from_fm(hT, hB, 16, ys[j * 16:(j + 1) * 16, :])
            evs.append(DMA("pool", "sto", sts[j].rearrange("h k v -> k h v"), Sst[:, :, :], SB, [stB], slow=True))
        for k in ["sto", "os0", "os1", "os2", "os3", "kts", "vss0", "vss1", "vss2", "vss3"]:
            if k in S.dma_cnt:
                S.wait_event("pool", (k, S.dma_cnt[k]))
        S.replay(block)
    return nc, S


_CONSTS = None


def _consts():
    global _CONSTS
    if _CONSTS is None:
        ident = np.eye(128, dtype=np.float32)
        scanm = np.ones((128, 512), np.float32)
        scanm[:, ::64] = 0.0
        s = np.arange(128)[:, None]
        t = np.arange(128)[None, :]
        m = ((s // 64 == t // 64) & (s <= t)).astype(np.float32)
        mask2 = np.tile(m, (1, 4))
        _CONSTS = (ident, scanm, mask2)
    return _CONSTS


_NC_CACHE = {}


def kernel(x_prompt, x_sample, cache_k, cache_v, state_hgrn, norm_g, w_hgrn_in, hgrn_lb_logits, hgrn_onorm_g, w_hgrn_out,
           kv_norm_g, w_kv, w_dq, diff_lambda, diff_subln_g, w_do, w_up, w_down):
    f = lambda a: np.ascontiguousarray(np.asarray(a), dtype=np.float32)
    x_prompt = f(x_prompt); x_sample = f(x_sample); cache_k = f(cache_k); cache_v = f(cache_v); state_hgrn = f(state_hgrn)
    B, SEQ, D = x_prompt.shape
    assert B == 8 and D == 1024 and SEQ % TT == 0
    if SEQ not in _NC_CACHE:
        _NC_CACHE[SEQ] = build_nc(SEQ)[0]
    nc = _NC_CACHE[SEQ]
    ident, scanm, mask2 = _consts()
    shared = dict(
        norm_g=f(norm_g).reshape(8, 1024), w_in=f(w_hgrn_in)[0], lbl=f(hgrn_lb_logits), onorm=f(hgrn_onorm_g)[0],
        w_ho=f(w_hgrn_out)[0], kvg=f(kv_norm_g), w_kv=f(w_kv), w_dq=f(w_dq)[0], dlam=f(diff_lambda).reshape(256),
        subg=f(diff_subln_g).reshape(128), w_do=f(w_do)[0], w_up=f(w_up), w_down=f(w_down),
        c_ident=ident, c_scanm=scanm, c_mask2=mask2)
    in_maps = []
    for c in range(8):
        m = dict(shared)
        m["xp"] = x_prompt[c]
        m["xs"] = x_sample[2 * c:2 * c + 2].reshape(32, 1024)
        m["ck"] = cache_k[2 * c:2 * c + 2].reshape(2, 1024, 1024)
        m["cv"] = cache_v[2 * c:2 * c + 2].reshape(2, 1024, 1024)
        m["sth"] = state_hgrn[0, 2 * c:2 * c + 2]
        in_maps.append(m)
    res = run_bass_kernel_spmd(nc, in_maps, core_ids=list(range(8)))
    r = res.results
    y_prompt = np.stack([r[c]["yp"] for c in range(8)]).reshape(8, SEQ, 1024)
    y_sample = np.concatenate([r[c]["ys"].reshape(2, 16, 1024) for c in range(8)])
    k_prompt = np.stack([r[c]["kp"] for c in range(8)]).reshape(8, SEQ, 8, 2, 64)
    v_prompt = np.stack([r[c]["vp"] for c in range(8)]).reshape(8, SEQ, 8, 128)
    st_prompt = np.stack([r[c]["stp"] for c in range(8)])[None]
    k_sample = np.concatenate([r[c]["ks"].reshape(2, 16, 8, 2, 64) for c in range(8)])
    v_sample = np.concatenate([r[c]["vs"].reshape(2, 16, 8, 128) for c in range(8)])
    st_sample = np.concatenate([r[c]["sts"] for c in range(8)])[None]
    return (y_prompt.astype(np.float32), y_sample.astype(np.float32), k_prompt.astype(np.float32), v_prompt.astype(np.float32),
            st_prompt.astype(np.float32), k_sample.astype(np.float32), v_sample.astype(np.float32), st_sample.astype(np.float32))
```

```python
import math
import os
KDBG = 'nopool'
KCUT = int(os.environ.get('KCUT', '99'))
KCUT2 = int(os.environ.get('KCUT2', '99'))
from contextlib import ExitStack
import numpy as np
import concourse.bass as bass
import concourse.mybir as mybir
from concourse.bass_utils import run_bass_kernel_spmd

F32 = mybir.dt.float32
BF16 = mybir.dt.bfloat16
AF = mybir.ActivationFunctionType
ALU = mybir.AluOpType
AX = mybir.AxisListType

TT = 512
EPS = 1e-6
LAM_INIT = 0.8 - 0.6 * math.exp(-0.3 * 1)
NW = 4
NPIECE = 50
STRICT = True


class Buf:
    __slots__ = ("name", "w", "r", "excl")

    def __init__(self, name, excl=False):
        self.name = name
        self.w = None
        self.r = []
        self.excl = excl


class Eng:
    def __init__(self, name, in_order_safe=False):
        self.name = name
        self.ops = []
        self.sem = name
        self.count = 0
        self.seen = {}
        self.in_order_safe = in_order_safe


class Sched:
    def __init__(self):
        self.E = {"pe": Eng("pe", True), "act": Eng("act"), "dve": Eng("dve"), "pool": Eng("pool"), "sp": Eng("sp")}
        self.sems = {}
        self.dma_cnt = {}
        self.n_ops = 0
        self.n_waits = 0

    def _need(self, eng, evs):
        need = {}
        for (k, v) in evs:
            if eng.seen.get(k, 0) >= v:
                continue
            if need.get(k, 0) < v:
                need[k] = v
        return need

    def _emit_waits(self, eng, need):
        for k, v in need.items():
            h = self.sems[k]
            eng.ops.append(lambda e, h=h, v=v: e.wait_ge(h, v))
            eng.seen[k] = v
            self.n_waits += 1

    def _record(self, ev, reads, writes):
        for b in reads:
            b.r.append(ev)
            if len(b.r) > 64:
                mx = {}
                for (k, v) in b.r:
                    if mx.get(k, 0) < v:
                        mx[k] = v
                b.r = list(mx.items())
        for b in writes:
            b.w = ev
            b.r = []

    def op(self, ename, fn, reads=(), writes=(), inc=True):
        if ename == "pool" and "nopool" in KDBG:
            ename = "dve"
        eng = self.E[ename]
        if ename != "pe":
            ex = [b for b in reads if b.excl]
            if ex:
                reads = [b for b in reads if not b.excl]
                writes = list(writes) + ex
        evs = []
        for b in reads:
            if b.w is not None:
                evs.append(b.w)
        for b in writes:
            if b.w is not None and (STRICT or b.w[0] != eng.sem):
                evs.append(b.w)
            for ev in b.r:
                if STRICT or ev[0] != eng.sem:
                    evs.append(ev)
        if eng.in_order_safe:
            evs = [e for e in evs if e[0] != eng.sem]
        self._emit_waits(eng, self._need(eng, evs))
        self.n_ops += 1
        if inc:
            eng.count += 1
            ev = (eng.sem, eng.count)
            h = self.sems[eng.sem]
            eng.ops.append(lambda e, fn=fn, h=h: fn(e).then_inc(h, 1))
        else:
            ev = (eng.sem, eng.count + 1)
            eng.ops.append(lambda e, fn=fn: fn(e))
        self._record(ev, reads, writes)
        return ev

    def dma(self, qname, semkey, fn, reads=(), writes=()):
        eng = self.E[qname]
        evs = []
        for b in reads:
            if b.w is not None:
                evs.append(b.w)
        for b in writes:
            if b.w is not None:
                evs.append(b.w)
            evs.extend(b.r)
        self._emit_waits(eng, self._need(eng, evs))
        self.dma_cnt[semkey] = self.dma_cnt.get(semkey, 0) + 16
        ev = (semkey, self.dma_cnt[semkey])
        h = self.sems[semkey]
        eng.ops.append(lambda e, fn=fn, h=h: fn(e).then_inc(h, 16))
        self.n_ops += 1
        self._record(ev, reads, writes)
        return ev

    def wait_event(self, ename, ev):
        eng = self.E[ename]
        self._emit_waits(eng, self._need(eng, [ev]))

    def replay(self, block):
        E = self.E

        @block.tensor
        def _(e):
            for f in E["pe"].ops:
                f(e)

        @block.scalar
        def _(e):
            for f in E["act"].ops:
                f(e)

        @block.vector
        def _(e):
            for f in E["dve"].ops:
                f(e)

        @block.gpsimd
        def _(e):
            for f in E["pool"].ops:
                f(e)

        @block.sync
        def _(e):
            for f in E["sp"].ops:
                f(e)


def build_nc(SEQ, STOP=99):
    NTILE = SEQ // TT
    nc = bass.Bass("TRN2", target_bir_lowering=False)

    def din(name, shape, dt=F32):
        return nc.dram_tensor(name, list(shape), dt, kind="ExternalInput").ap()

    def dout(name, shape):
        return nc.dram_tensor(name, list(shape), F32, kind="ExternalOutput").ap()

    xp = din("xp", [SEQ, 1024]); xs = din("xs", [32, 1024])
    ck = din("ck", [2, 1024, 1024]); cv = din("cv", [2, 1024, 1024]); sth = din("sth", [2, 8, 128, 128])
    norm_g = din("norm_g", [8, 1024]); w_in = din("w_in", [1024, 4096]); lbl = din("lbl", [2, 1024])
    onorm = din("onorm", [1024]); w_ho = din("w_ho", [1024, 1024]); kvg = din("kvg", [1024])
    w_kv = din("w_kv", [1024, 2048]); w_dq = din("w_dq", [1024, 1024]); dlam = din("dlam", [256])
    subg = din("subg", [128]); w_do = din("w_do", [1024, 1024])
    w_up = din("w_up", [2, 1024, 4096]); w_down = din("w_down", [2, 4096, 1024])
    c_ident = din("c_ident", [128, 128]); c_scanm = din("c_scanm", [128, 512]); c_mask2 = din("c_mask2", [128, 512])

    yp = dout("yp", [SEQ, 1024]); ys = dout("ys", [32, 1024]); kp = dout("kp", [SEQ, 1024]); vp = dout("vp", [SEQ, 1024])
    stp = dout("stp", [8, 128, 128]); ks = dout("ks", [32, 1024]); vs = dout("vs", [32, 1024]); sts = dout("sts", [2, 8, 128, 128])

    wbf = nc.dram_tensor("wbf", [NPIECE, 128, 4096], BF16, kind="Internal").ap()
    KTs = nc.dram_tensor("KTs", [8, 128, SEQ], BF16, kind="Internal").ap()
    Vs = nc.dram_tensor("Vs", [8, 128, SEQ // 128, 128], BF16, kind="Internal").ap()

    pieces = []
    for cg in range(8):
        pieces.append((w_in, 0, cg))
    for cg in range(2):
        pieces.append((w_ho, 0, cg))
    for cg in range(8):
        pieces.append((w_up[0], 0, cg))
    for cg in range(2):
        for kg in range(4):
            pieces.append((w_down[0], kg, cg))
    for cg in range(4):
        pieces.append((w_kv, 0, cg))
    for cg in range(2):
        pieces.append((w_dq, 0, cg))
    for cg in range(2):
        pieces.append((w_do, 0, cg))
    for cg in range(8):
        pieces.append((w_up[1], 0, cg))
    for cg in range(2):
        for kg in range(4):
            pieces.append((w_down[1], kg, cg))
    assert len(pieces) == NPIECE
    tile_seq = ([0, 1, 2, 3, 6, 7, 4, 5, 8, 9] + list(range(10, 26)) + [26, 27, 28, 29, 26, 27, 30, 31, 32, 33]
                + list(range(34, 50)))
    n_tiles_total = NTILE + 2
    use_seq = tile_seq * n_tiles_total

    with ExitStack() as st:
        def sb(name, shape, dt):
            return st.enter_context(nc.sbuf_tensor(name, list(shape), dt))

        ident_f = sb("ident_f", [128, 128], F32); ident_b = sb("ident_b", [128, 128], BF16)
        ones_f = sb("ones_f", [128, 128], F32); ones_b = sb("ones_b", [128, 128], BF16); onesD = sb("onesD", [128, 128], BF16); onesH = sb("onesH", [128, 128], BF16)
        scanm = sb("scanm", [128, 512], F32); mask2 = sb("mask2", [128, 512], F32)
        G = sb("G", [128, 80], F32); LB = sb("LB", [128, 64], F32); SG = sb("SG", [128, 4], F32)
        DL = sb("DL", [128, 512], F32); epst = sb("epst", [128, 1], F32)
        hT = sb("hT", [128, 8, 512], F32); aT = sb("aT", [128, 8, 512], BF16); aT2 = sb("aT2", [128, 8, 512], BF16)
        sqs = sb("sqs", [128, 2, 512], BF16); rstd = sb("rstd", [128, 2, 512], F32)
        mT = sb("mT", [128, 8, 512], F32)
        ost = sb("ost", [128, 4, 1024], F32)
        Vtok = sb("Vtok", [128, 4, 1024], BF16)
        R = sb("R", [128, 16, 1024], BF16)
        TH = sb("TH", [128, 2, 4, 512], F32)
        QK = sb("QK", [128, 2, 3, 512], BF16)
        ATm = sb("ATm", [128, 2, 512], BF16); Ktok = sb("Ktok", [128, 2, 512], BF16)
        Sst = sb("Sst", [128, 8, 128], F32); Sall = sb("Sall", [128, 9, 128], F32); Sbf = sb("Sbf", [128, 2, 8, 128], BF16)
        KT = sb("KT", [128, 8, 512], BF16); QT = sb("QT", [128, 8, 512], BF16)
        Wt = sb("Wt", [128, NW, 4096], BF16)
        PSQ = [st.enter_context(nc.psum_tensor(f"psq{i}", [128, 1024], F32)) for i in range(4)]

        S = Sched()
        semnames = (["pe", "act", "dve", "pool", "sp", "cst", "xld", "kts", "vss0", "vss1", "vss2", "vss3", "stl", "sto", "cvl0", "cvl1"]
                    + [f"w{i}" for i in range(NW)] + [f"ws{i}" for i in range(NW)]
                    + [f"rk{i}" for i in range(4)] + [f"rv{i}" for i in range(4)] + [f"os{i}" for i in range(4)])
        S.sems = {n: st.enter_context(nc.semaphore(n)) for n in semnames}
        block = st.enter_context(nc.Block())

        hB = [Buf(f"h{i}") for i in range(8)]; aB = [Buf(f"a{i}") for i in range(8)]; a2B = [Buf(f"a2{i}") for i in range(8)]
        mB = [Buf(f"m{i}") for i in range(8)]; sqB = [Buf("sq0"), Buf("sq1")]; rsB = [Buf("rs0"), Buf("rs1")]
        ostB = [Buf(f"ost{i}") for i in range(4)]; vtB = [Buf(f"vt{i}") for i in range(4)]
        RB = [Buf(f"R{i}") for i in range(16)]
        THB = [[Buf(f"TH{p}{i}") for i in range(4)] for p in range(2)]
        QKB = [[Buf(f"QK{p}{i}") for i in range(3)] for p in range(2)]
        ATB = [Buf("AT0"), Buf("AT1")]; KtB = [Buf("Kt0"), Buf("Kt1")]
        SB = [Buf(f"S{i}") for i in range(8)]; SallB = Buf("Sall"); SbfB = [Buf("Sbf0"), Buf("Sbf1")]
        KTB = [Buf(f"KT{i}") for i in range(8)]; QB = [Buf(f"Q{i}") for i in range(8)]
        WB = [Buf(f"W{i}") for i in range(NW)]
        BP = [Buf(f"bank{i}", excl=True) for i in range(8)]
        cB = Buf("consts")
        wbfB = [Buf(f"wbf{i}") for i in range(NPIECE)]
        ktsB = [Buf(f"kts{i}") for i in range(max(NTILE, 1))]; vsB = [Buf(f"vs{i}") for i in range(max(NTILE, 1))]
        stB = Buf("stout")

        def bank(i):
            return PSQ[i // 2][:, (i % 2) * 512:(i % 2) * 512 + 512]

        bank_rr = [0]

        def nextbank():
            b = bank_rr[0]
            bank_rr[0] = (b + 1) % 7
            return b

        def MM(out, lhsT, rhs, start, stop, reads, writes, inc=True, skip=False):
            if skip:
                S.op("pe", lambda e: e.matmul(out, lhsT, rhs, start=start, stop=stop, skip_group_check=True), reads, writes, inc)
            else:
                S.op("pe", lambda e: e.matmul(out, lhsT, rhs, start=start, stop=stop), reads, writes, inc)

        def TR(out, in_, ident, reads, writes, inc=True):
            S.op("pe", lambda e: e.transpose(out, in_, ident), reads, writes, inc)

        def ACT(out, in_, func, reads, writes, scale=1.0, bias=None):
            if bias is None:
                S.op("act", lambda e: e.activation(out=out, in_=in_, func=func, scale=scale), reads, writes)
            else:
                S.op("act", lambda e: e.activation(out=out, in_=in_, func=func, scale=scale, bias=bias), reads, writes)

        def CP(eng, out, in_, reads, writes):
            if eng == "act":
                S.op("act", lambda e: e.copy(out, in_), reads, writes)
            else:
                S.op(eng, lambda e: e.tensor_copy(out, in_), reads, writes)

        def TTo(eng, out, in0, in1, op, reads, writes):
            S.op(eng, lambda e: e.tensor_tensor(out=out, in0=in0, in1=in1, op=op), reads, writes)

        def TS(eng, out, in0, s1, s2, op0, op1, reads, writes):
            S.op(eng, lambda e: e.tensor_scalar(out=out, in0=in0, scalar1=s1, scalar2=s2, op0=op0, op1=op1), reads, writes)

        def STT(eng, out, in0, scalar, in1, op0, op1, reads, writes):
            eng = "dve"
            S.op(eng, lambda e: e.scalar_tensor_tensor(out=out, in0=in0, scalar=scalar, in1=in1, op0=op0, op1=op1), reads, writes)

        def MEMSET(eng, ap, val, writes):
            S.op(eng, lambda e: e.memset(ap, val), (), writes)

        def DMA(q, sem, out, in_, reads, writes, slow=False):
            if slow:
                return S.dma(q, sem, lambda e: e.dma_start(out=out, in_=in_, allow_slow_non_contiguous=True), reads, writes)
            return S.dma(q, sem, lambda e: e.dma_start(out=out, in_=in_), reads, writes)

        MEMSET("pool", ones_f[:], 1.0, [cB]); MEMSET("pool", ones_b[:], 1.0, [cB]); MEMSET("pool", onesD[:], 1.0 / 1024.0, [cB]); MEMSET("pool", onesH[:], 1.0 / 128.0, [cB])
        MEMSET("pool", epst[:], EPS, [cB])
        ncst = 0
        DMA("sp", "cst", ident_f[:], c_ident[:, :], [], [cB]); ncst += 1
        DMA("sp", "cst", scanm[:], c_scanm[:, :], [], [cB]); ncst += 1
        DMA("sp", "cst", mask2[:], c_mask2[:, :], [], [cB]); ncst += 1
        DMA("sp", "cst", G[:, 0:64].rearrange("p (r c) -> p r c", c=8), norm_g.rearrange("r (c p) -> p r c", p=128), [], [cB], slow=True); ncst += 1
        DMA("sp", "cst", G[:, 64:72], onorm.rearrange("(c p) -> p c", p=128), [], [cB], slow=True); ncst += 1
        DMA("sp", "cst", G[:, 72:80], kvg.rearrange("(c p) -> p c", p=128), [], [cB], slow=True); ncst += 1
        DMA("sp", "cst", LB[:, 0:16].rearrange("p (l c) -> p l c", c=8), lbl.rearrange("l (c p) -> p l c", p=128), [], [cB], slow=True); ncst += 1
        DMA("sp", "cst", SG[:, 0:1], subg.rearrange("(p o) -> p o", o=1), [], [cB], slow=True); ncst += 1
        DMA("sp", "cst", DL[:, 0:256], bass.AP(dlam.tensor, 0, [[0, 128], [1, 256]]), [], [cB], slow=True); ncst += 1
        cB.w = ("cst", 16 * ncst)
        cB.r = []
        CP("dve", ident_b[:], ident_f[:], [cB], [cB])
        ACT(LB[:, 0:16], LB[:, 0:16], AF.Exp, [cB], [cB])
        TTo("dve", LB[:, 16:24], LB[:, 0:8], LB[:, 8:16], ALU.add, [cB], [cB])
        S.op("dve", lambda e: e.reciprocal(LB[:, 16:24], LB[:, 16:24]), [cB], [cB])
        TTo("dve", LB[:, 24:32], LB[:, 0:8], LB[:, 16:24], ALU.mult, [cB], [cB])
        TTo("dve", LB[:, 32:40], LB[:, 8:16], LB[:, 16:24], ALU.mult, [cB], [cB])
        S.op("dve", lambda e: e.tensor_scalar_mul(LB[:, 40:48], LB[:, 32:40], -1.0), [cB], [cB])
        TTo("dve", DL[:, 256:320], DL[:, 0:64], DL[:, 64:128], ALU.mult, [cB], [cB])
        TTo("dve", DL[:, 320:384], DL[:, 128:192], DL[:, 192:256], ALU.mult, [cB], [cB])
        S.op("dve", lambda e: e.reduce_sum(DL[:, 384:386], DL[:, 256:384].rearrange("p (a b) -> p a b", b=64), AX.X), [cB], [cB])
        ACT(DL[:, 384:386], DL[:, 384:386], AF.Exp, [cB], [cB])
        TTo("dve", SG[:, 1:2], DL[:, 385:386], DL[:, 384:385], ALU.subtract, [cB], [cB])
        S.op("dve", lambda e: e.tensor_scalar_add(SG[:, 1:2], SG[:, 1:2], -LAM_INIT), [cB], [cB])
        S.op("dve", lambda e: e.tensor_scalar_mul(SG[:, 2:3], SG[:, 0:1], 1.0 - LAM_INIT), [cB], [cB])
        lb_ap = lambda h: LB[:, 24 + h:25 + h]
        oml_ap = lambda h: LB[:, 32 + h:33 + h]
        noml_ap = lambda h: LB[:, 40 + h:41 + h]
        neg_lam = SG[:, 1:2]; subgs = SG[:, 2:3]

        def gcol(r, c):
            return G[:, r * 8 + c:r * 8 + c + 1]

        stg = [hT[:, :, :], mT[:, :, :]]
        stgB = [hB, mB]
        cast_eng = ["dve", "act", "pool"]
        for i, (wap, kg, cg) in enumerate(pieces):
            sI = i % 2
            src = wap[kg * 1024:(kg + 1) * 1024, cg * 512:(cg + 1) * 512].rearrange("(kc p) j -> p kc j", p=128)
            DMA("sp", f"cvl{sI}", stg[sI], src, [], stgB[sI])
            slot = i % NW
            wv = Wt[:, slot, :].rearrange("p (kc j) -> p kc j", j=512)
            CP(cast_eng[i % 3], wv, stg[sI], stgB[sI], [WB[slot]])
            DMA("pool", f"ws{slot}", wbf[i], Wt[:, slot, :], [WB[slot]], [wbfB[i]])

        ring = {"issued": 0, "cur": 0}

        def ring_issue(i):
            slot = i % NW
            DMA("sp", f"w{slot}", Wt[:, slot, :], wbf[use_seq[i]], [wbfB[use_seq[i]]], [WB[slot]])

        def next_piece(expect):
            while ring["issued"] < min(len(use_seq), ring["cur"] + NW):
                ring_issue(ring["issued"])
                ring["issued"] += 1
            i = ring["cur"]
            assert use_seq[i] == expect, (i, use_seq[i], expect)
            ring["cur"] += 1
            slot = i % NW
            return Wt[:, slot, :].rearrange("p (kc j) -> p kc j", j=512), WB[slot]

        def proj_fm(inT, inB, N, piece_ids, evac):
            for pi, pid in enumerate(piece_ids):
                wap, wB = next_piece(pid)
                for o in range(4):
                    b = nextbank()
                    for kc in range(8):
                        MM(bank(b)[:, :N], wap[:, kc, o * 128:(o + 1) * 128], inT[:, kc, :N], kc == 0, kc == 7,
                           [wB, inB[kc]], [BP[b]], inc=(kc == 7))
                    evac(pi * 4 + o, bank(b)[:, :N], BP[b])

        def proj_down(uT, uB, N, piece_base, evac):
            for cg in range(2):
                bs = [nextbank() for _ in range(4)]
                for kg in range(4):
                    wap, wB = next_piece(piece_base + cg * 4 + kg)
                    for o in range(4):
                        for kc in range(8):
                            j = kg * 8 + kc
                            MM(bank(bs[o])[:, :N], wap[:, kc, o * 128:(o + 1) * 128], uT(j)[:, :N],
                               (kg == 0 and kc == 0), (kg == 3 and kc == 7), [wB, uB(j)], [BP[bs[o]]],
                               inc=(kc == 7))
                for o in range(4):
                    evac(cg * 4 + o, bank(bs[o])[:, :N], BP[bs[o]])

        def proj_tm(inT, inB, N, pid, evac):
            wap, wB = next_piece(pid)
            nsub = (N + 127) // 128
            subr = min(N, 128)
            for s in range(nsub):
                b = nextbank()
                for kc in range(8):
                    MM(bank(b)[:subr, :], inT[:, kc, s * 128:s * 128 + subr], wap[:, kc, :], kc == 0, kc == 7,
                       [wB, inB[kc]], [BP[b]], inc=(kc == 7))
                evac(s, bank(b)[:subr, :], BP[b])

        def norm_rstd(chunks, ones_ap, N, ri, psum_src=True):
            b = nextbank()
            n = len(chunks)
            for c, (ap, Bs) in enumerate(chunks):
                sl = c % 2
                if psum_src or c % 2 == 0:
                    ACT(sqs[:, sl, :N], ap, AF.Square, Bs, [sqB[sl]])
                else:
                    TTo("pool", sqs[:, sl, :N], ap, ap, ALU.mult, Bs, [sqB[sl]])
                MM(bank(b)[:, :N], ones_ap[:, :], sqs[:, sl, :N], c == 0, c == n - 1, [sqB[sl], cB], [BP[b]], inc=True)
            ACT(rstd[:, ri, :N], bank(b)[:, :N], AF.Ln, [BP[b], cB], [rsB[ri]], bias=epst[:, 0:1])
            ACT(rstd[:, ri, :N], rstd[:, ri, :N], AF.Exp, [rsB[ri]], [rsB[ri]], scale=-0.5)

        def h_chunks(N):
            return [(hT[:, c, :N], [hB[c]]) for c in range(8)]

        def m_chunks(N):
            return [(mT[:, c, :N], [mB[c]]) for c in range(8)]

        def apply_norm(dstT, dstB, grow, N, ri):
            for c in range(8):
                eng = "dve" if c % 2 == 0 else "pool"
                STT(eng, dstT[:, c, :N], hT[:, c, :N], gcol(grow, c), rstd[:, ri, :N], ALU.mult, ALU.mult,
                    [hB[c], rsB[ri], cB], [dstB[c]])

        def resid_add(grow, N):
            ACT(rstd[:, 0, :N], bank(7)[:, :N], AF.Ln, [BP[7], cB], [rsB[0]], bias=epst[:, 0:1])
            ACT(rstd[:, 0, :N], rstd[:, 0, :N], AF.Exp, [rsB[0]], [rsB[0]], scale=-0.5)
            for c in range(8):
                eng = "dve" if c % 2 == 0 else "pool"
                TTo(eng, mT[:, c, :N], mT[:, c, :N], rstd[:, 0, :N], ALU.mult, [mB[c], rsB[0]], [mB[c]])
                STT(eng, hT[:, c, :N], mT[:, c, :N], gcol(grow, c), hT[:, c, :N], ALU.mult, ALU.add,
                    [mB[c], hB[c], cB], [hB[c]])

        def stat_mm(idx, N):
            sl = idx % 2
            MM(bank(7)[:, :N], onesD[:, :], sqs[:, sl, :N], idx == 0, idx == 7, [sqB[sl], cB], [BP[7]], inc=True)

        def evac_to_m(idx, ps, pB, N):
            ACT(sqs[:, idx % 2, :N], ps, AF.Square, [pB], [sqB[idx % 2]])
            if idx >= 1:
                stat_mm(idx - 1, N)
            if idx == 7:
                stat_mm(7, N)
            if idx % 2 == 0:
                CP("act", mT[:, idx, :N], ps, [pB], [mB[idx]])
            else:
                CP("dve", mT[:, idx, :N], ps, [pB], [mB[idx]])

        def Rf32(slot):
            return R[:, slot, :].bitcast(F32)

        def mlp(layer, N, up_base, down_base):
            norm_rstd(h_chunks(N), onesD, N, 0)
            apply_norm(aT, aB, layer * 4 + 2, N, 0)

            def ev_up(j, ps, pB):
                slot = j // 2
                half = (j % 2) * 512
                tmp = TH[:, j % 2, 0, :].bitcast(BF16)[:, 0:512]
                ACT(tmp[:, :N], ps, AF.Relu, [pB], [THB[j % 2][0]])
                TTo("pool" if j % 2 == 0 else "dve", R[:, slot, half:half + N], tmp[:, :N], tmp[:, :N], ALU.mult,
                    [THB[j % 2][0]], [RB[slot]])
            proj_fm(aT, aB, N, list(range(up_base, up_base + 8)), ev_up)
            proj_down(lambda j: R[:, j // 2, (j % 2) * 512:(j % 2) * 512 + 512], lambda j: RB[j // 2], N, down_base,
                      lambda idx, ps, pB: evac_to_m(idx, ps, pB, N))
            resid_add(layer * 4 + 3, N)

        def load_x_dma(src_rows, N):
            nsub = (N + 127) // 128
            subr = min(N, 128)
            xv = mT[:, :, :].rearrange("p a b -> p (a b)").rearrange("p (s f) -> p s f", f=1024)
            if N >= 128:
                DMA("sp", "xld", xv[:, :nsub, :], src_rows.rearrange("(s p) f -> p s f", p=128), [], mB)
            else:
                DMA("sp", "xld", xv[:subr, 0, :], src_rows, [], mB)

        def load_x(src_rows, N, dma_done=False):
            nsub = (N + 127) // 128
            subr = min(N, 128)
            xv = mT[:, :, :].rearrange("p a b -> p (a b)").rearrange("p (s f) -> p s f", f=1024)
            if not dma_done:
                load_x_dma(src_rows, N)
            for c in range(8):
                b = nextbank()
                for s in range(nsub):
                    TR(bank(b)[:, s * 128:s * 128 + subr], xv[:subr, s, c * 128:(c + 1) * 128], ident_f[:subr, :subr],
                       mB + [cB], [BP[b]], inc=(s == nsub - 1))
                CP("act" if c % 2 == 0 else "dve", hT[:, c, :N], bank(b)[:, :N], [BP[b]], [hB[c]])

        def store_rows_from_fm(srcT, srcB, N, dst_rows):
            nsub = (N + 127) // 128
            subr = min(N, 128)
            for s in range(nsub):
                slot = ost_rr[0]
                ost_rr[0] = (slot + 1) % 4
                for half in range(2):
                    b = nextbank()
                    for cc in range(4):
                        c = half * 4 + cc
                        TR(bank(b)[:subr, cc * 128:(cc + 1) * 128], srcT[:, c, s * 128:s * 128 + subr], ident_f[:, :],
                           [srcB[c], cB], [BP[b]], inc=(cc == 3))
                    CP("act" if half == 0 else "dve", ost[:subr, slot, half * 512:(half + 1) * 512], bank(b)[:subr, :],
                       [BP[b]], [ostB[slot]])
                DMA("pool", f"os{slot}", dst_rows[s * 128:s * 128 + subr, :], ost[:subr, slot, :], [ostB[slot]], [])

        ost_rr = [0]

        def hgrn(N, CH):
            npair = (N + 127) // 128
            pw = min(N, 128)
            nch = N // CH
            norm_rstd(h_chunks(N), onesD, N, 0)
            apply_norm(aT, aB, 0, N, 0)
            def ev_q(j, ps, pB):
                ACT(R[:, 8 + j // 2, (j % 2) * 512:(j % 2) * 512 + N], ps, AF.Silu, [pB], [RB[8 + j // 2]])
            proj_fm(aT, aB, N, [0, 1], ev_q)
            def ev_f(j, ps, pB):
                ACT(Rf32(j)[:, :N], ps, AF.Sigmoid, [pB], [RB[j]])
            proj_fm(aT, aB, N, [2, 3], ev_f)
            def ev_g(j, ps, pB):
                ACT(R[:, 12 + j // 2, (j % 2) * 512:(j % 2) * 512 + N], ps, AF.Silu, [pB], [RB[12 + j // 2]])
            proj_fm(aT, aB, N, [6, 7], ev_g)
            subr = min(N, 128)
            for half in range(2):
                def ev_v(s, ps, pB, half=half):
                    CP("dve" if s % 2 == 0 else "act", Vtok[:subr, s, half * 512:(half + 1) * 512], ps, [pB], [vtB[s]])
                proj_tm(aT, aB, N, 4 + half, ev_v)

            def stageA(h):
                pb = h % 2
                T = lambda i: TH[:, pb, i, :N]
                TB = THB[pb]
                sig = Rf32(h)[:, :N]
                sq_ = R[:, 8 + h // 2, (h % 2) * 512:(h % 2) * 512 + N]
                ACT(T(0), sig, AF.Ln, [RB[h], cB], [TB[0]], scale=oml_ap(h), bias=lb_ap(h))
                ACT(T(1), sig, AF.Identity, [RB[h], cB], [TB[1]], scale=noml_ap(h), bias=oml_ap(h))
                S.op("dve", lambda e: e.tensor_tensor_scan(out=T(2), data0=scanm[:, :N], data1=T(0), initial=0.0,
                                                           op0=ALU.mult, op1=ALU.add), [TB[0], cB], [TB[2]])
                ACT(T(0), T(2), AF.Exp, [TB[2]], [TB[0]])
                ACT(T(3), T(2), AF.Exp, [TB[2]], [TB[3]], scale=-1.0)
                Qt = QK[:, pb, 0, :N]; Kt = QK[:, pb, 1, :N]; Kh = QK[:, pb, 2, :N]
                TTo("dve", Qt, sq_, T(0), ALU.mult, [RB[8 + h // 2], TB[0]], [QKB[pb][0]])
                TTo("pool", Kt, T(1), T(3), ALU.mult, [TB[1], TB[3]], [QKB[pb][1]])
                Ev = TH[:, pb, 0, :N].rearrange("p (c t) -> p c t", t=CH)
                TTo("pool", Kh.rearrange("p (c t) -> p c t", t=CH), Kt.rearrange("p (c t) -> p c t", t=CH),
                    Ev[:, :, CH - 1:CH].broadcast_to([128, nch, CH]), ALU.mult, [QKB[pb][1], TB[0]], [QKB[pb][2]])

            def stageA2(h):
                pb = h % 2
                Qt = QK[:, pb, 0, :N]; Kt = QK[:, pb, 1, :N]; Kh = QK[:, pb, 2, :N]
                for j in range(npair):
                    MM(bank(0)[:pw, j * 128:j * 128 + pw], Kt[:, j * 128:j * 128 + pw], Qt[:, j * 128:j * 128 + pw], True, True,
                       [QKB[pb][0], QKB[pb][1]], [BP[0]], inc=(j == npair - 1))
                TTo("dve", ATm[:pw, pb, :N], bank(0)[:pw, :N], mask2[:pw, :N], ALU.mult, [BP[0], cB], [ATB[pb]])
                pT = bank(1).bitcast(BF16)
                for j in range(npair):
                    TR(pT[:pw, j * 128:(j + 1) * 128], Kh[:, j * 128:j * 128 + pw], ident_b[:, :], [QKB[pb][2], cB], [BP[1]],
                       inc=(j == npair - 1))
                CP("act", Ktok[:pw, pb, :npair * 128], pT[:pw, :npair * 128], [BP[1]], [KtB[pb]])

            def stageA3(h):
                pb = h % 2
                TB = THB[pb]
                dsl = lambda c: PSQ[1][:, (c % 2) * 512 + (c // 2) * 128:(c % 2) * 512 + (c // 2) * 128 + 128]
                for c in range(nch):
                    j = (c * CH) // 128
                    r0 = (c * CH) % 128
                    bb = 2 + c % 2
                    MM(dsl(c), Ktok[r0:r0 + CH, pb, j * 128:(j + 1) * 128],
                       Vtok[r0:r0 + CH, j, h * 128:(h + 1) * 128], True, True, [KtB[pb], vtB[j]], [BP[bb]],
                       inc=(c >= nch - 2))
                CP("pool", Sall[:, 0, :], Sst[:, h, :], [SB[h]], [SallB])
                for c in range(nch):
                    bb = 2 + c % 2
                    STT("dve", Sall[:, c + 1, :], Sall[:, c, :], TH[:, pb, 0, c * CH + CH - 1:c * CH + CH],
                        dsl(c), ALU.mult, ALU.add, [SallB, TB[0], BP[bb]], [SallB])
                CP("pool", Sst[:, h, :], Sall[:, nch, :], [SallB], [SB[h]])
                CP("act", Sbf[:, pb, :nch, :], Sall[:, 0:nch, :], [SallB], [SbfB[pb]])

            def stageB(h):
                pb = h % 2
                Qt = QK[:, pb, 0, :N]
                for j in range(npair):
                    MM(bank(4)[:, j * 128:j * 128 + pw], Vtok[:pw, j, h * 128:(h + 1) * 128], ATm[:pw, pb, j * 128:j * 128 + pw],
                       j == 0, False, [vtB[j], ATB[pb]], [BP[4]], inc=False, skip=True)
                for c in range(nch):
                    MM(bank(4)[:, c * CH:(c + 1) * CH], Sbf[:, pb, c, :], Qt[:, c * CH:(c + 1) * CH], False, c == nch - 1,
                       [SbfB[pb], QKB[pb][0]], [BP[4]], inc=(c == nch - 1), skip=True)
                ACT(sqs[:, 0, :N], bank(4)[:, :N], AF.Square, [BP[4]], [sqB[0]])
                MM(bank(5)[:, :N], onesH[:, :], sqs[:, 0, :N], True, True, [sqB[0], cB], [BP[5]])
                ACT(rstd[:, 1, :N], bank(5)[:, :N], AF.Ln, [BP[5], cB], [rsB[1]], bias=epst[:, 0:1])
                ACT(rstd[:, 1, :N], rstd[:, 1, :N], AF.Exp, [rsB[1]], [rsB[1]], scale=-0.5)

            def stageB2(h):
                pb = h % 2
                tmp = TH[:, pb, 2, :N]
                STT("dve", tmp, bank(4)[:, :N], G[:, 64 + h:65 + h], rstd[:, 1, :N], ALU.mult, ALU.mult,
                    [BP[4], rsB[1], cB], [THB[pb][2]])
                sg_ = R[:, 12 + h // 2, (h % 2) * 512:(h % 2) * 512 + N]
                TTo("pool", aT2[:, h, :N], tmp, sg_, ALU.mult, [THB[pb][2], RB[12 + h // 2]], [a2B[h]])

            stageA(0); stageA2(0)
            for h in range(8):
                if h + 1 < 8:
                    stageA(h + 1)
                if h >= 1:
                    stageB2(h - 1)
                stageA3(h)
                if h + 1 < 8:
                    stageA2(h + 1)
                stageB(h)
            stageB2(7)
            proj_fm(aT2, a2B, N, [8, 9], lambda idx, ps, pB: evac_to_m(idx, ps, pB, N))
            resid_add(1, N)

        def attend(h, N, kblocks):
            nkb = len(kblocks)

            Pbufs = [(TH[:, 0, 0, :].bitcast(BF16), [THB[0][0]]), (TH[:, 1, 0, :].bitcast(BF16), [THB[1][0]]),
                     (QK[:, 0, 0:2, :].rearrange("p a b -> p (a b)"), [QKB[0][0], QKB[0][1]])]

            def qk(i):
                kb = kblocks[i]
                par = i % 2
                q = PSQ[par]
                cl = kb["cl"]; nk = kb["nk"]
                P, PBl = Pbufs[i % 3]
                MM(q[:nk, cl:N], kb["KT"][0:64, :nk], QT[0:64, h, cl:N], True, True, kb["rk"] + [QB[h]], [BP[2 * par]], inc=False)
                MM(q[:nk, 512 + cl:512 + N], kb["KT"][64:128, :nk], QT[64:128, h, cl:N], True, True, kb["rk"] + [QB[h]],
                   [BP[2 * par + 1]], inc=True)
                qv = q[:nk, :].rearrange("p (two n) -> p two n", two=2)[:, :, cl:N]
                pv = P[:nk, :].rearrange("p (two n) -> p two n", two=2)[:, :, cl:N]
                ACT(pv, qv, AF.Exp, [BP[2 * par], BP[2 * par + 1]], PBl, scale=0.125)
                if kb["corner"]:
                    cv_ = P[64:128, :].rearrange("p (two n) -> p two n", two=2)[:, :, cl:cl + 64]
                    MEMSET("pool", cv_, 0.0, PBl)

            acc2 = TH[:, 1, 3, :]

            def pv(i):
                kb = kblocks[i]
                cl = kb["cl"]; nk = kb["nk"]
                P, PBl = Pbufs[i % 3]
                first = (i == 0); last = (i == nkb - 1)
                MM(bank(4)[:, cl:N], kb["V"][:nk, :], P[:nk, cl:N], first, last, kb["rv"] + PBl, [BP[4]], inc=False)
                MM(bank(5)[:, cl:N], kb["V"][:nk, :], P[:nk, 512 + cl:512 + N], first, last, kb["rv"] + PBl, [BP[5]], inc=False)
                MM(bank(6)[:, cl:N], ones_b[:nk, :], P[:nk, cl:N], first, last, PBl + [cB], [BP[6]], inc=True)
                if first:
                    CP("dve", acc2[:nk, cl:N], P[:nk, 512 + cl:512 + N], PBl, [THB[1][3]])
                else:
                    TTo("dve", acc2[:nk, cl:N], acc2[:nk, cl:N], P[:nk, 512 + cl:512 + N], ALU.add, PBl + [THB[1][3]], [THB[1][3]])
                if last:
                    MM(bank(7)[:, :N], ones_f[:, :], acc2[:, :N], True, True, [THB[1][3], cB], [BP[7]], inc=True)

            qk(0)
            if nkb > 1:
                qk(1)
            for i in range(nkb):
                if i + 2 < nkb:
                    qk(i + 2)
                pv(i)
            r1 = TH[:, 0, 1, :N]; r2 = TH[:, 0, 2, :N]; o1 = TH[:, 1, 1, :N]; o2 = TH[:, 1, 2, :N]
            ACT(r1, bank(6)[:, :N], AF.Ln, [BP[6]], [THB[0][1]])
            ACT(r2, bank(7)[:, :N], AF.Ln, [BP[7]], [THB[0][2]])
            CP("dve", o1, bank(4)[:, :N], [BP[4]], [THB[1][1]])
            CP("dve", o2, bank(5)[:, :N], [BP[5]], [THB[1][2]])
            ACT(r1, r1, AF.Exp, [THB[0][1]], [THB[0][1]], scale=-1.0)
            ACT(r2, r2, AF.Exp, [THB[0][2]], [THB[0][2]], scale=-1.0)
            TTo("dve", o1, o1, r1, ALU.mult, [THB[1][1], THB[0][1]], [THB[1][1]])
            TTo("dve", o2, o2, r2, ALU.mult, [THB[1][2], THB[0][2]], [THB[1][2]])
            STT("dve", mT[:, h, :N], o2, neg_lam, o1, ALU.mult, ALU.add, [THB[1][1], THB[1][2], cB], [mB[h]])

        def attn_subln(N):
            for h in range(8):
                ACT(sqs[:, h % 2, :N], mT[:, h, :N], AF.Square, [mB[h]], [sqB[h % 2]])
                MM(bank(h)[:, :N], onesH[:, :], sqs[:, h % 2, :N], True, True, [sqB[h % 2], cB], [BP[h]])
            for h in range(8):
                rr = TH[:, h % 2, 3, :N]
                ACT(rr, bank(h)[:, :N], AF.Ln, [BP[h], cB], [THB[h % 2][3]], bias=epst[:, 0:1])
                ACT(rr, rr, AF.Exp, [THB[h % 2][3]], [THB[h % 2][3]], scale=-0.5)
                STT("dve", aT[:, h, :N], mT[:, h, :N], subgs, rr, ALU.mult, ALU.mult, [mB[h], THB[h % 2][3], cB], [aB[h]])

        def diff_layer(N, mode, t=0, j=0):
            nsub = (N + 127) // 128
            subr = min(N, 128)
            norm_rstd(h_chunks(N), onesD, N, 0)
            apply_norm(aT, aB, 4, N, 0)
            apply_norm(aT2, a2B, 9, N, 0)
            kdst = kp[t * TT:(t + 1) * TT, :] if mode == "prompt" else ks[j * 16:(j + 1) * 16, :]
            vdst = vp[t * TT:(t + 1) * TT, :] if mode == "prompt" else vs[j * 16:(j + 1) * 16, :]
            for which, dst in ((0, kdst), (1, vdst)):
                for half in range(2):
                    def ev_kv(s, ps, pB, which=which, half=half):
                        eng = "act" if (s + half) % 2 == 0 else "dve"
                        CP(eng, ost[:subr, s, half * 512:(half + 1) * 512], ps, [pB], [ostB[s]])
                        if which == 1:
                            CP("dve" if eng == "act" else "act", Vtok[:subr, s, half * 512:(half + 1) * 512],
                               ost[:subr, s, half * 512:(half + 1) * 512], [ostB[s]], [vtB[s]])
                    proj_tm(aT2, a2B, N, 26 + which * 2 + half, ev_kv)
                for s in range(nsub):
                    DMA("pool", f"os{s}", dst[s * 128:s * 128 + subr, :], ost[:subr, s, :], [ostB[s]], [])
            if KCUT2 < 2:
                ring["cur"] += 6; ring["issued"] = max(ring["issued"], ring["cur"])
                return
            def ev_k(c, ps, pB):
                CP("act" if c % 2 == 0 else "dve", KT[:, c, :N], ps, [pB], [KTB[c]])
            proj_fm(aT2, a2B, N, [26, 27], ev_k)
            def ev_qq(c, ps, pB):
                CP("dve" if c % 2 == 0 else "act", QT[:, c, :N], ps, [pB], [QB[c]])
            proj_fm(aT, aB, N, [30, 31], ev_qq)
            if mode == "prompt":
                if t < NTILE - 1:
                    DMA("pool", "kts", KTs[:, :, t * TT:(t + 1) * TT].rearrange("h p n -> p h n"), KT[:, :, :], KTB, [ktsB[t]], slow=True)
                    for s in range(4):
                        DMA("pool", f"vss{s}", Vs[:, :, t * 4 + s, :].rearrange("h p e -> p h e"),
                            Vtok[:, s, :].rearrange("p (h e) -> p h e", h=8), [vtB[s]], [vsB[t]], slow=True)
                npast = t * TT
                nblk = (npast + 2047) // 2048
                for h in range(8):
                    kbs = []
                    for blk in range(nblk):
                        rs = (h * nblk + blk) % 4
                        k0 = blk * 2048
                        k1 = min(npast, k0 + 2048)
                        nkb_ = (k1 - k0) // 128
                        Kr = R[:, 4 * rs:4 * rs + 2, :].rearrange("p a b -> p (a b)")
                        Vr = R[:, 4 * rs + 2:4 * rs + 4, :].rearrange("p a b -> p (a b)").rearrange("p (k e) -> p k e", e=128)
                        tl = list(range(k0 // TT, (k1 + TT - 1) // TT))
                        DMA("sp", f"rk{rs}", Kr[:, :k1 - k0], KTs[h, :, k0:k1], [ktsB[x] for x in tl], [RB[4 * rs], RB[4 * rs + 1]])
                        DMA("sp", f"rv{rs}", Vr[:, :nkb_, :], Vs[h, :, k0 // 128:k1 // 128, :], [vsB[x] for x in tl],
                            [RB[4 * rs + 2], RB[4 * rs + 3]])
                        for kb_ in range(nkb_):
                            kbs.append(dict(KT=Kr[:, kb_ * 128:(kb_ + 1) * 128], V=Vr[:, kb_, :], nk=128, cl=0, corner=False,
                                            rk=[RB[4 * rs], RB[4 * rs + 1]], rv=[RB[4 * rs + 2], RB[4 * rs + 3]]))
                    for jj in range(4):
                        kbs.append(dict(KT=KT[:, h, jj * 128:(jj + 1) * 128], V=Vtok[:, jj, h * 128:(h + 1) * 128], nk=128,
                                        cl=128 * jj, corner=True, rk=[KTB[h]], rv=[vtB[jj]]))
                    attend(h, N, kbs)
                attn_subln(N)
            else:
                xv = mT[:, :, :].rearrange("p a b -> p (a b)").rearrange("p (s f) -> p s f", f=1024)
                for half in range(2):
                    DMA("sp", "xld", xv, ck[j, half * 512:(half + 1) * 512, :].rearrange("(s p) f -> p s f", p=128), [], mB)
                    for c in range(8):
                        b = nextbank()
                        for s in range(4):
                            TR(bank(b)[:, s * 128:(s + 1) * 128], xv[:, s, c * 128:(c + 1) * 128], ident_f[:, :], mB + [cB], [BP[b]],
                               inc=(s == 3))
                        CP("act" if c % 2 == 0 else "dve", R[:, c, half * 512:(half + 1) * 512], bank(b), [BP[b]], [RB[c]])
                for half in range(2):
                    DMA("sp", "xld", xv, cv[j, half * 512:(half + 1) * 512, :].rearrange("(s p) f -> p s f", p=128), [], mB)
                    for s in range(4):
                        CP("dve" if s % 2 == 0 else "pool", R[:, 8 + half * 4 + s, :], xv[:, s, :], mB, [RB[8 + half * 4 + s]])
                for h in range(8):
                    kbs = []
                    for kb_ in range(8):
                        kbs.append(dict(KT=R[:, h, kb_ * 128:(kb_ + 1) * 128], V=R[:, 8 + kb_, h * 128:(h + 1) * 128], nk=128, cl=0,
                                        corner=False, rk=[RB[h]], rv=[RB[8 + kb_]]))
                    kbs.append(dict(KT=KT[:, h, 0:N], V=Vtok[:N, 0, h * 128:(h + 1) * 128], nk=N, cl=0, corner=False,
                                    rk=[KTB[h]], rv=[vtB[0]]))
                    attend(h, N, kbs)
                attn_subln(N)
            proj_fm(aT, aB, N, [32, 33], lambda idx, ps, pB: evac_to_m(idx, ps, pB, N))
            resid_add(5, N)

        for h in range(8):
            MEMSET("pool", Sst[:, h, :], 0.0, [SB[h]])
        for t in range(NTILE):
            if STOP < 2:
                break
            load_x(xp[t * TT:(t + 1) * TT, :], TT, dma_done=(t > 0))
            if STOP < 3:
                store_rows_from_fm(hT, hB, TT, yp[t * TT:(t + 1) * TT, :])
                continue
            hgrn(TT, 64)
            if STOP < 4:
                ring["cur"] += 40; ring["issued"] = max(ring["issued"], ring["cur"])
                store_rows_from_fm(hT, hB, TT, yp[t * TT:(t + 1) * TT, :])
                continue
            mlp(0, TT, 10, 18)
            if STOP < 5:
                ring["cur"] += 24; ring["issued"] = max(ring["issued"], ring["cur"])
                store_rows_from_fm(hT, hB, TT, yp[t * TT:(t + 1) * TT, :])
                continue
            diff_layer(TT, "prompt", t=t)
            if KCUT2 < 2:
                ring["cur"] += 16; ring["issued"] = max(ring["issued"], ring["cur"])
                store_rows_from_fm(hT, hB, TT, yp[t * TT:(t + 1) * TT, :])
                continue
            mlp(1, TT, 34, 42)
            if t + 1 < NTILE:
                load_x_dma(xp[(t + 1) * TT:(t + 2) * TT, :], TT)
            store_rows_from_fm(hT, hB, TT, yp[t * TT:(t + 1) * TT, :])
        evs = [DMA("pool", "sto", stp.rearrange("h k v -> k h v"), Sst[:, :, :], SB, [stB], slow=True)]
        for j in range(2):
            if STOP < 6:
                break
            DMA("sp", "stl", Sst[:, :, :], sth[j].rearrange("h k v -> k h v"), [stB], SB, slow=True)
            load_x(xs[j * 16:(j + 1) * 16, :], 16)
            hgrn(16, 16)
            mlp(0, 16, 10, 18)
            diff_layer(16, "sample", j=j)
            mlp(1, 16, 34, 42)
            store_rows_from_fm(hT, hB, 16, ys[j * 16:(j + 1) * 16, :])
            evs.append(DMA("pool", "sto", sts[j].rearrange("h k v -> k h v"), Sst[:, :, :], SB, [stB], slow=True))
        for k in ["sto", "os0", "os1", "os2", "os3", "kts", "vss0", "vss1", "vss2", "vss3"]:
            if k in S.dma_cnt:
                S.wait_event("pool", (k, S.dma_cnt[k]))
        S.replay(block)
    return nc, S


_CONSTS = None


def _consts():
    global _CONSTS
    if _CONSTS is None:
        ident = np.eye(128, dtype=np.float32)
        scanm = np.ones((128, 512), np.float32)
        scanm[:, ::64] = 0.0
        s = np.arange(128)[:, None]
        t = np.arange(128)[None, :]
        m = ((s // 64 == t // 64) & (s <= t)).astype(np.float32)
        mask2 = np.tile(m, (1, 4))
        _CONSTS = (ident, scanm, mask2)
    return _CONSTS


_NC_CACHE = {}


def kernel(x_prompt, x_sample, cache_k, cache_v, state_hgrn, norm_g, w_hgrn_in, hgrn_lb_logits, hgrn_onorm_g, w_hgrn_out,
           kv_norm_g, w_kv, w_dq, diff_lambda, diff_subln_g, w_do, w_up, w_down):
    f = lambda a: np.ascontiguousarray(np.asarray(a), dtype=np.float32)
    x_prompt = f(x_prompt); x_sample = f(x_sample); cache_k = f(cache_k); cache_v = f(cache_v); state_hgrn = f(state_hgrn)
    B, SEQ, D = x_prompt.shape
    assert B == 8 and D == 1024 and SEQ % TT == 0
    if SEQ not in _NC_CACHE:
        _NC_CACHE[SEQ] = build_nc(SEQ)[0]
    nc = _NC_CACHE[SEQ]
    ident, scanm, mask2 = _consts()
    shared = dict(
        norm_g=f(norm_g).reshape(8, 1024), w_in=f(w_hgrn_in)[0], lbl=f(hgrn_lb_logits), onorm=f(hgrn_onorm_g)[0],
        w_ho=f(w_hgrn_out)[0], kvg=f(kv_norm_g), w_kv=f(w_kv), w_dq=f(w_dq)[0], dlam=f(diff_lambda).reshape(256),
        subg=f(diff_subln_g).reshape(128), w_do=f(w_do)[0], w_up=f(w_up), w_down=f(w_down),
        c_ident=ident, c_scanm=scanm, c_mask2=mask2)
    in_maps = []
    for c in range(8):
        m = dict(shared)
        m["xp"] = x_prompt[c]
        m["xs"] = x_sample[2 * c:2 * c + 2].reshape(32, 1024)
        m["ck"] = cache_k[2 * c:2 * c + 2].reshape(2, 1024, 1024)
        m["cv"] = cache_v[2 * c:2 * c + 2].reshape(2, 1024, 1024)
        m["sth"] = state_hgrn[0, 2 * c:2 * c + 2]
        in_maps.append(m)
    res = run_bass_kernel_spmd(nc, in_maps, core_ids=list(range(8)))
    r = res.results
    y_prompt = np.stack([r[c]["yp"] for c in range(8)]).reshape(8, SEQ, 1024)
    y_sample = np.concatenate([r[c]["ys"].reshape(2, 16, 1024) for c in range(8)])
    k_prompt = np.stack([r[c]["kp"] for c in range(8)]).reshape(8, SEQ, 8, 2, 64)
    v_prompt = np.stack([r[c]["vp"] for c in range(8)]).reshape(8, SEQ, 8, 128)
    st_prompt = np.stack([r[c]["stp"] for c in range(8)])[None]
    k_sample = np.concatenate([r[c]["ks"].reshape(2, 16, 8, 2, 64) for c in range(8)])
    v_sample = np.concatenate([r[c]["vs"].reshape(2, 16, 8, 128) for c in range(8)])
    st_sample = np.concatenate([r[c]["sts"] for c in range(8)])[None]
    return (y_prompt.astype(np.float32), y_sample.astype(np.float32), k_prompt.astype(np.float32), v_prompt.astype(np.float32),
            st_prompt.astype(np.float32), k_sample.astype(np.float32), v_sample.astype(np.float32), st_sample.astype(np.float32))
```
